# Optimizing a Trainium2 kernel written in Bass

```python
import math
import jax, jax.numpy as jnp
from jax import lax
import numpy as np

D_MODEL = 1024
BATCH = 8
SEQ = 2048
DEPTH = 1

N_META = 16
C_CONV = D_MODEL
CONV_WIDTH = 31
N_HEADS = 16
HEAD_DIM = 64
ATTN_W = N_HEADS * HEAD_DIM
Q_BLOCK = 128
D_FF = ((8 * D_MODEL // 3 + 255) // 256) * 256
N_BRANCH = 2
IN_W = 2 * C_CONV + 3 * ATTN_W + N_BRANCH * D_MODEL
RMS_EPS = 1e-6
LN_EPS = 1e-5

kernel_name = "hybrid_conformer_stickbreaking_block"


def rms_norm(x, g):
    xf = x.astype(jnp.float32)
    y = xf * lax.rsqrt(jnp.mean(xf * xf, axis=-1, keepdims=True) + RMS_EPS)
    return (y * g.astype(jnp.float32)).astype(x.dtype)


def layer_norm(x, g, b):
    xf = x.astype(jnp.float32)
    mu = jnp.mean(xf, axis=-1, keepdims=True)
    xc = xf - mu
    var = jnp.mean(xc * xc, axis=-1, keepdims=True)
    y = xc * lax.rsqrt(var + LN_EPS) * g.astype(jnp.float32) + b.astype(jnp.float32)
    return y.astype(x.dtype)


def conformer_conv(u_glu, dw_w, dw_b, ln_g, ln_b, w_out):
    a, gate = jnp.split(u_glu, 2, axis=-1)
    u = a * jax.nn.sigmoid(gate)
    y = lax.conv_general_dilated(
        u, dw_w[:, None, :], window_strides=(1,),
        padding=[(CONV_WIDTH - 1, 0)],
        dimension_numbers=("NWC", "WIO", "NWC"),
        feature_group_count=C_CONV) + dw_b
    y = jax.nn.silu(layer_norm(y, ln_g, ln_b))
    return y @ w_out


def stick_breaking_block(q_blk, k_pre, v_pre, q_start):
    nq, nk = q_blk.shape[1], k_pre.shape[1]
    scale = 1.0 / math.sqrt(HEAD_DIM)
    z = jnp.einsum("bqhd,bkhd->bhqk", q_blk.astype(jnp.float32),
                   k_pre.astype(jnp.float32)) * scale
    t = q_start + jnp.arange(nq)
    s = jnp.arange(nk)
    mask = s[None, :] < t[:, None]
    sp = jnp.where(mask, jax.nn.softplus(z), 0.0)
    r = lax.cumsum(sp, axis=3, reverse=True)
    log_a = jnp.where(mask, z - r, -jnp.inf)
    a = jnp.exp(log_a)
    out = jnp.einsum("bhqk,bkhd->bqhd", a, v_pre.astype(jnp.float32))
    return out.astype(q_blk.dtype)


def stick_breaking_attention(q, k, v):
    L = q.shape[1]
    n_real_blocks = (L - N_META) // Q_BLOCK
    bounds = [(0, N_META)] + [(N_META + i * Q_BLOCK, N_META + (i + 1) * Q_BLOCK)
                              for i in range(n_real_blocks)]
    outs = []
    for (st, en) in bounds:
        outs.append(stick_breaking_block(q[:, st:en], k[:, :en], v[:, :en], st))
    return jnp.concatenate(outs, axis=1)


def setup_inputs(seed: int = 0) -> dict:
    key = jax.random.key(seed)
    ks = jax.random.split(key, 20)
    f32 = jnp.float32
    nrm = lambda k, shape, s: jax.random.normal(k, shape, f32) * s
    gain = lambda k, shape: 1.0 + 0.02 * jax.random.normal(k, shape, f32)
    return {
        "x": jax.random.normal(ks[0], (BATCH, SEQ, D_MODEL), f32),
        "meta_tokens": nrm(ks[1], (N_META, D_MODEL), 1.0),
        "pre_mix_g": gain(ks[2], (DEPTH, D_MODEL)),
        "w_in": nrm(ks[3], (DEPTH, D_MODEL, IN_W), D_MODEL ** -0.5),
        "gate_b": nrm(ks[4], (DEPTH, N_BRANCH * D_MODEL), 0.02),
        "dw_w": nrm(ks[5], (DEPTH, CONV_WIDTH, C_CONV), CONV_WIDTH ** -0.5),
        "dw_b": nrm(ks[6], (DEPTH, C_CONV), 0.02),
        "conv_ln_g": gain(ks[7], (DEPTH, C_CONV)),
        "conv_ln_b": nrm(ks[8], (DEPTH, C_CONV), 0.02),
        "w_conv_out": nrm(ks[9], (DEPTH, C_CONV, D_MODEL), C_CONV ** -0.5),
        "w_attn_out": nrm(ks[10], (DEPTH, ATTN_W, D_MODEL), ATTN_W ** -0.5),
        "w_o": nrm(ks[11], (DEPTH, D_MODEL, D_MODEL), D_MODEL ** -0.5),
        "post_mix_g": gain(ks[12], (DEPTH, D_MODEL)),
        "pre_ffn_g": gain(ks[13], (DEPTH, D_MODEL)),
        "w_ffn_in": nrm(ks[14], (DEPTH, D_MODEL, 2 * D_FF), D_MODEL ** -0.5),
        "w_ffn_out": nrm(ks[15], (DEPTH, D_FF, D_MODEL), D_FF ** -0.5),
        "post_ffn_g": gain(ks[16], (DEPTH, D_MODEL)),
    }


def reference(x, meta_tokens, pre_mix_g, w_in, gate_b, dw_w, dw_b, conv_ln_g, conv_ln_b,
              w_conv_out, w_attn_out, w_o, post_mix_g, pre_ffn_g, w_ffn_in, w_ffn_out,
              post_ffn_g):
    B = x.shape[0]
    meta = jnp.broadcast_to(meta_tokens[None].astype(x.dtype), (B, N_META, D_MODEL))
    h = jnp.concatenate([meta, x], axis=1)
    L = h.shape[1]
    for l in range(DEPTH):
        u = rms_norm(h, pre_mix_g[l])
        p = u @ w_in[l]
        o1 = 2 * C_CONV
        o2 = o1 + ATTN_W
        o3 = o2 + ATTN_W
        o4 = o3 + ATTN_W
        p_glu = p[..., :o1]
        q = p[..., o1:o2].reshape(B, L, N_HEADS, HEAD_DIM)
        k = p[..., o2:o3].reshape(B, L, N_HEADS, HEAD_DIM)
        v = p[..., o3:o4].reshape(B, L, N_HEADS, HEAD_DIM)
        gates = jax.nn.sigmoid(p[..., o4:] + gate_b[l])
        g_conv, g_attn = jnp.split(gates, 2, axis=-1)

        y_conv = conformer_conv(p_glu, dw_w[l], dw_b[l], conv_ln_g[l], conv_ln_b[l],
                                w_conv_out[l])
        y_attn = stick_breaking_attention(q, k, v).reshape(B, L, ATTN_W) @ w_attn_out[l]

        mix = (g_conv * y_conv + g_attn * y_attn) @ w_o[l]
        h = h + rms_norm(mix, post_mix_g[l])

        u = rms_norm(h, pre_ffn_g[l])
        a, b = jnp.split(u @ w_ffn_in[l], 2, axis=-1)
        f = (jax.nn.silu(a) * b) @ w_ffn_out[l]
        h = h + rms_norm(f, post_ffn_g[l])
    return h[:, N_META:]
```

```python
import numpy as np
import concourse.bass as bass
import concourse.mybir as mybir
from concourse.bass_utils import run_bass_kernel_spmd

F32 = mybir.dt.float32
BF16 = mybir.dt.bfloat16
AF = mybir.ActivationFunctionType
ALU = mybir.AluOpType

D = 1024
SEQ = 2048
NM = 16
L = SEQ + NM
NH = 16
DH = 64
DFF = 2816
KC = D // 128
FC = DFF // 128
CW = 31
RMS_EPS = 1e-6
LN_EPS = 1e-5
NCORES = 8
ENGS = ("sync", "scalar", "vector", "gpsimd", "tensor")


class Prog:
    def __init__(self):
        self.ops = []

    def add(self, eng, fn, reads=(), writes=(), dma=False):
        self.ops.append((eng, fn, tuple(reads) + ("BAR",), tuple(writes), dma))

    def barrier(self, fn):
        self.ops.append(("gpsimd", fn, (), ("BAR",), False))

    def analyze(self):
        ops = self.ops
        last_w = {}
        readers = {}
        pos = []
        cnt = {e: 0 for e in ENGS}
        for (eng, _, _, _, _) in ops:
            pos.append(cnt[eng])
            cnt[eng] += 1
        final = []
        needed = set()
        for i, (eng, fn, reads, writes, dma) in enumerate(ops):
            deps = set()
            raw = set()
            for r in reads:
                if r in last_w:
                    deps.add(last_w[r])
                    raw.add(last_w[r])
            for w in writes:
                if w in last_w:
                    deps.add(last_w[w])
                deps.update(readers.get(w, ()))
            deps.discard(i)
            for r in reads:
                readers.setdefault(r, []).append(i)
            for w in writes:
                last_w[w] = i
                readers[w] = []
            best = {}
            keep = set()
            for d in deps:
                deng, _, _, _, ddma = ops[d]
                if ddma:
                    keep.add(d)
                    continue
                if deng == eng and not dma:
                    if d not in raw or eng == "tensor":
                        continue
                if deng not in best or pos[best[deng]] < pos[d]:
                    best[deng] = d
            keep.update(best.values())
            final.append(keep)
            needed |= keep
        self.final = final
        self.needed = needed

    def emit(self, csem, dsems):
        self.analyze()
        ops = self.ops
        sem_of = {}
        ccount = {e: 0 for e in ENGS}
        dcount = {e: [0] * len(dsems[e]) for e in dsems}
        dk = {e: 0 for e in dsems}
        dma_prev = {}
        for i, (eng, fn, reads, writes, dma) in enumerate(ops):
            if dma:
                pool = dsems[eng]
                s = dk[eng] % len(pool)
                dk[eng] += 1
                dma_prev[i] = (pool[s], dcount[eng][s])
                dcount[eng][s] += 16
                sem_of[i] = (pool[s], dcount[eng][s])
            elif i in self.needed:
                ccount[eng] += 1
                sem_of[i] = (csem[eng], ccount[eng])
        streams = {e: [] for e in ENGS}
        for i, op in enumerate(ops):
            streams[op[0]].append(i)
        self.stats = {e: len(streams[e]) for e in ENGS}

        def run(engname, eng):
            wd = {}
            for i in streams[engname]:
                _, fn, _, _, dma = ops[i]
                waits = {}
                for d in self.final[i]:
                    s, v = sem_of[d]
                    k = id(s)
                    if k not in waits or waits[k][1] < v:
                        waits[k] = (s, v)
                if dma:
                    s, v = dma_prev[i]
                    if v > 0:
                        k = id(s)
                        if k not in waits or waits[k][1] < v:
                            waits[k] = (s, v)
                for k, (s, v) in waits.items():
                    if wd.get(k, 0) >= v:
                        continue
                    wd[k] = v
                    eng.wait_ge(s, v)
                ins = fn(eng)
                if i in sem_of:
                    ins.then_inc(sem_of[i][0], 16 if dma else 1)
            if engname == "sync":
                for e in dsems:
                    for s, c in zip(dsems[e], dcount[e]):
                        if c > 0:
                            eng.wait_ge(s, c)
        return run


def build_program(debug=False):
    nc = bass.Bass("TRN2", target_bir_lowering=False)

    def din(name, shape, dt=F32):
        return nc.dram_tensor(name, list(shape), dt, kind="ExternalInput").ap()

    x = din("x", [SEQ, D])
    meta = din("meta", [NM, D])
    w_in = din("w_in", [D, 7 * D])
    w_conv_out = din("w_conv_out", [D, D])
    w_attn_out = din("w_attn_out", [D, D])
    w_o = din("w_o", [D, D])
    w_ffn_in = din("w_ffn_in", [D, 2 * DFF])
    w_ffn_out = din("w_ffn_out", [DFF, D])
    g_pre = din("g_pre", [D])
    g_postmix = din("g_postmix", [D])
    g_preffn = din("g_preffn", [D])
    g_postffn = din("g_postffn", [D])
    pvec_d = din("pvec", [128, 40])
    dwT_d = din("dwT", [128, KC * CW])
    out = nc.dram_tensor("out", [SEQ, D], F32, kind="ExternalOutput").ap()
    wffin_bf = nc.dram_tensor("wffin_bf", [FC // 2, 128, KC * 512], BF16, kind="Internal").ap()
    wo_bf = nc.dram_tensor("wo_bf", [2, 128, KC * 512], BF16, kind="Internal").ap()

    def slabview(t3, i):
        return t3[i].rearrange("p (kc j) -> p kc j", kc=KC)
    dbg = {}
    if debug:
        for nm in ("ya", "ycs", "mix"):
            dbg[nm] = nc.dram_tensor("dbg_" + nm, [128, KC, SEQ], BF16, kind="ExternalOutput").ap()
        dbg["uT"] = nc.dram_tensor("dbg_uT", [128, KC, L], BF16, kind="ExternalOutput").ap()
        dbg["q"] = nc.dram_tensor("dbg_q", [128, KC, SEQ], BF16, kind="ExternalOutput").ap()
        dbg["k"] = nc.dram_tensor("dbg_k", [128, KC, L], BF16, kind="ExternalOutput").ap()
        dbg["v"] = nc.dram_tensor("dbg_v", [128, 17, D], BF16, kind="ExternalOutput").ap()
        dbg["yc"] = nc.dram_tensor("dbg_yc", [128, KC, SEQ], F32, kind="ExternalOutput").ap()

    P = Prog()

    SB_BASE = 16512

    def sb(name, shape, dt, off):
        assert off + SB_BASE < 229376
        return nc.alloc_sbuf_tensor_at(name, list(shape), dt, offset=off + SB_BASE)

    o = 0
    ident = sb("ident", [128, 128], BF16, o); o += 256
    negtri = sb("negtri", [128, 128], BF16, o); o += 256
    negones = sb("negones", [128, 128], BF16, o); o += 256
    zeros = sb("zeros", [128, 128], BF16, o); o += 256
    onesf = sb("onesf", [128, 128], F32, o); o += 512
    MB = sb("MB", [128, 5, 512], BF16, o); o += 5120
    pvec = sb("pvec", [128, 40], F32, o); o += 160
    dwT = sb("dwT", [128, KC * CW], F32, o); o += 992
    ss = sb("ss", [128, 32], F32, o); o += 128
    lnt = sb("lnt", [128, 32], F32, o); o += 128
    rstd = sb("rstd", [128, 32], F32, o); o += 128
    st = sb("st", [128, 64], F32, o); o += 256
    epsr = sb("epsr", [128, 1], F32, o); o += 32
    epsl = sb("epsl", [128, 1], F32, o); o += 32
    barj = sb("barj", [128, 8], F32, o); o += 32
    assert o <= 9216
    PERS = 9216

    pp = [nc.alloc_psum_tensor("pp%d" % i, [128, 1024], F32) for i in range(4)]

    def bank(i):
        return pp[i // 2][:, (i % 2) * 512:(i % 2) * 512 + 512]

    def bank_bf(i):
        return pp[i // 2][:, (i % 2) * 512:(i % 2) * 512 + 512].bitcast(BF16).rearrange("p (c n) -> p c n", c=8)

    def MM(outp, lhsT, rhs, start, stop, reads, writes, skip=False):
        P.add("tensor", lambda e: e.matmul(outp, lhsT, rhs, start=start, stop=stop, skip_group_check=skip), reads, writes)

    def TR(outp, in_, idn, reads, writes):
        P.add("tensor", lambda e: e.transpose(outp, in_, idn), reads, writes)

    def ACT(outp, in_, func, reads, writes, **kw):
        P.add("scalar", lambda e: e.activation(out=outp, in_=in_, func=func, **kw), reads, writes)

    def TT(eng, outp, in0, in1, op, reads, writes):
        P.add(eng, lambda e: e.tensor_tensor(out=outp, in0=in0, in1=in1, op=op), reads, writes)

    def TS(eng, outp, in0, s1, s2, op0, op1, reads, writes):
        if s2 is None:
            P.add(eng, lambda e: e.tensor_scalar(out=outp, in0=in0, scalar1=s1, scalar2=None, op0=op0), reads, writes)
        else:
            P.add(eng, lambda e: e.tensor_scalar(out=outp, in0=in0, scalar1=s1, scalar2=s2, op0=op0, op1=op1), reads, writes)

    def STT(eng, outp, in0, scalar, in1, op0, op1, reads, writes):
        P.add(eng, lambda e: e.scalar_tensor_tensor(out=outp, in0=in0, scalar=scalar, in1=in1, op0=op0, op1=op1), reads, writes)

    def CP(eng, outp, in_, reads, writes):
        if eng == "scalar":
            ACT(outp, in_, AF.Copy, reads, writes)
        else:
            P.add(eng, lambda e: e.tensor_copy(out=outp, in_=in_), reads, writes)

    def DMA(eng, outp, in_, reads, writes):
        P.add(eng, lambda e: e.dma_start(out=outp, in_=in_), reads, writes, dma=True)

    def MEMSET(eng, ap, val, reads, writes):
        P.add(eng, lambda e: e.memset(ap, val), reads, writes)

    def ASEL(outp, in_, pattern, cmp, fill, base, cm, reads, writes):
        P.add("gpsimd", lambda e: e.affine_select(out=outp, in_=in_, pattern=pattern, compare_op=cmp, fill=fill,
                                                 base=base, channel_multiplier=cm), reads, writes)

    def BARRIER():
        P.barrier(lambda e: e.memset(barj[:], 0.0))

    def wslab(src2d):
        return src2d.rearrange("(kc p) f -> p kc f", p=128)

    MEMSET("gpsimd", ident[:], 1.0, [], ["c_ident"])
    ASEL(ident[:], ident[:], [[-1, 128]], ALU.is_equal, 0.0, 0, 1, ["c_ident"], ["c_ident"])
    MEMSET("gpsimd", epsr[:], RMS_EPS, [], ["c_eps"])
    MEMSET("gpsimd", epsl[:], LN_EPS, [], ["c_eps2"])
    MEMSET("gpsimd", ss[:], 0.0, [], ["c_ss"])
    MEMSET("gpsimd", st[:], 0.0, [], ["c_st"])
    DMA("sync", pvec[:], pvec_d, [], ["c_pvec"])
    DMA("sync", dwT[:], dwT_d, [], ["c_dwT"])
    BARRIER()

    gate_b = lambda f: pvec[:, f:f + 1]
    dw_b = lambda c: pvec[:, 16 + c:17 + c]
    ln_g = lambda c: pvec[:, 24 + c:25 + c]
    ln_b = lambda c: pvec[:, 32 + c:33 + c]

    def tile_rows(t):
        return NM if t == 0 else 128

    def tile_cols(t):
        return (0, NM) if t == 0 else (NM + 128 * (t - 1), NM + 128 * t)

    CG = [(0, NM)] + [(NM + 512 * i, NM + 512 * (i + 1)) for i in range(4)]

    evac_rr = [0]

    def evac_eng():
        evac_rr[0] += 1
        return "scalar" if evac_rr[0] % 2 else "vector"

    def phase_A(first, gbc, xt, xn, sqj, uT, trbanks):
        DMA("sync", gbc[:], g_pre.partition_broadcast(128), [], ["gbc"])

        def stage1(t):
            rows = tile_rows(t)
            b = t % len(xt)
            src = meta if t == 0 else x[128 * (t - 1):128 * t, :]
            DMA("sync", xt[b][:rows, :], src, [], [("xt", b)])
            if first:
                ACT(sqj[:rows, :], xt[b][:rows, :], AF.Square, [("xt", b)], ["sqj", ("ss", t)], accum_out=ss[:rows, t:t + 1])
                ACT(lnt[:rows, t:t + 1], ss[:rows, t:t + 1], AF.Ln, [("ss", t)], [("lnt", t)], scale=1.0 / D, bias=epsr[:rows, :])
                ACT(rstd[:rows, t:t + 1], lnt[:rows, t:t + 1], AF.Exp, [("lnt", t)], [("rstd", t)], scale=-0.5)
            STT("vector", xn[b][:rows, :], xt[b][:rows, :], rstd[:rows, t:t + 1], gbc[:rows, :], ALU.mult, ALU.mult,
                [("xt", b), ("rstd", t), "gbc"], [("xn", b)])

        def stage2(t):
            rows = tile_rows(t)
            c0, c1 = tile_cols(t)
            b = t % len(xt)
            tb = trbanks[t % len(trbanks)]
            for c in range(KC):
                TR(bank_bf(tb)[:, c, :rows], xn[b][:rows, c * 128:(c + 1) * 128], ident[:rows, :rows], [("xn", b)], [("ps", tb)])
            CP("vector", uT[:, :, c0:c1], bank_bf(tb)[:, :, :rows], [("ps", tb)], [("uT", t)])

        stage1(0)
        stage1(1)
        for t in range(17):
            if t + 2 < 17:
                stage1(t + 2)
            stage2(t)

    uT = sb("uT", [128, KC, L], BF16, PERS)
    o = PERS + 33024
    A_LOC = o
    gbc1 = sb("gbc1", [128, D], F32, o); o += 4096
    xt1 = [sb("xt1_%d" % i, [128, D], F32, o + 4096 * i) for i in range(4)]; o += 16384
    xn1 = [sb("xn1_%d" % i, [128, D], BF16, o + 2048 * i) for i in range(4)]; o += 8192
    sqj1 = sb("sqj1", [128, D], BF16, o); o += 2048
    A_END = o

    phase_A(True, gbc1, xt1, xn1, sqj1, uT, [6, 7])
    MEMSET("gpsimd", negtri[:], -1.0, [], ["c_negtri"])
    ASEL(negtri[:], negtri[:], [[-1, 128]], ALU.is_ge, 0.0, 0, 1, ["c_negtri"], ["c_negtri"])
    MEMSET("gpsimd", negones[:], -1.0, [], ["c_negones"])
    MEMSET("gpsimd", zeros[:], 0.0, [], ["c_zeros"])
    MEMSET("gpsimd", onesf[:], 1.0, [], ["c_onesf"])
    for j in range(5):
        MEMSET("gpsimd", MB[:, j, :], 0.0, [], [("c_MB", j)])
        ASEL(MB[:, j, :], MB[:, j, :], [[1, 512]], ALU.is_gt, -30000.0, 16 - 128 * j, -1, [("c_MB", j)], [("c_MB", j)])
    if debug:
        DMA("sync", dbg["uT"], uT[:], [("uT", t) for t in range(17)], [])
    allu = [("uT", t) for t in range(17)]

    def cgkeys(cg):
        return [("uT", 0)] if cg == 0 else [("uT", t) for t in range(4 * (cg - 1) + 1, 4 * cg + 1)]
    pr = [0]

    def nextbank(n=6):
        pr[0] = (pr[0] + 1) % n
        return pr[0]

    N_PE, N_DVE, N_POOL = 23, 8, 0
    assert N_PE + N_DVE + N_POOL == CW
    TAP_PE = list(range(0, N_PE))
    TAP_DVE = list(range(N_PE, N_PE + N_DVE))
    TAP_POOL = list(range(N_PE + N_DVE, CW))
    o = A_END
    yc = sb("yc", [128, KC, SEQ], F32, o); o += 65536
    UB = 14 + L
    ubf = [sb("ubf%d" % i, [128, UB], BF16, o + 4160 * i) for i in range(2)]; o += 8320
    Dg = [sb("Dg%d" % i, [128, N_PE, 128], BF16, o + 256 * N_PE * i) for i in range(2)]; o += 2 * 256 * N_PE
    sig = [sb("sig%d" % i, [128, 512], F32, o + 2048 * i) for i in range(2)]; o += 4096
    wsl3 = [sb("wsl3_%d" % i, [128, KC, 256], BF16, o + 4096 * i) for i in range(2)]; o += 8192
    ydve = [sb("ydve%d" % i, [128, 512], F32, o + 2048 * i) for i in range(2)]; o += 4096
    if N_POOL:
        ypool = [sb("ypool%d" % i, [128, 512], F32, o + 2048 * i) for i in range(2)]; o += 4096
        ptmp = sb("ptmp", [128, 512], F32, o); o += 2048
    B1_END = o
    assert o <= 212800, o

    for b in range(2):
        MEMSET("gpsimd", ubf[b][:, 0:14], 0.0, [], [("ubf", b)])
    rr = [0]

    def b1_load(c):
        b = c % 2
        DMA("gpsimd", wsl3[b][:, :, 0:128], wslab(w_in[:, c * 128:(c + 1) * 128]), [], [("wsl", b)])
        DMA("gpsimd", wsl3[b][:, :, 128:256], wslab(w_in[:, D + c * 128:D + (c + 1) * 128]), [], [("wsl", b)])
        for i_, j in enumerate(TAP_PE):
            ACT(Dg[b][:, i_, :], ident[:], AF.Copy, [], [("Dg", b)], scale=dwT[:, c * CW + j:c * CW + j + 1])

    def b1_glu(c, cg):
        b = c % 2
        c0, c1 = CG[cg]
        n = c1 - c0
        ba = nextbank()
        for kc in range(KC):
            MM(bank(ba)[:, :n], wsl3[b][:, kc, 0:128], uT[:, kc, c0:c1], kc == 0, kc == KC - 1, [("wsl", b)] + cgkeys(cg), [("ps", ba)])
        bg = nextbank()
        for kc in range(KC):
            MM(bank(bg)[:, :n], wsl3[b][:, kc, 128:256], uT[:, kc, c0:c1], kc == 0, kc == KC - 1, [("wsl", b)] + cgkeys(cg), [("ps", bg)])
        r = rr[0] % 2
        rr[0] += 1
        ACT(sig[r][:, :n], bank(bg)[:, :n], AF.Sigmoid, [("ps", bg)], [("sig", r)])
        TT("vector", ubf[b][:, 14 + c0:14 + c1], bank(ba)[:, :n], sig[r][:, :n], ALU.mult, [("ps", ba), ("sig", r)], [("ubf", b)])

    def b1_conv(c, g):
        b = c % 2
        r = (4 * c + g) % 2
        gs = slice(512 * g, 512 * g + 512)
        win = lambda j: ubf[b][:, 512 * g + j:512 * g + j + 512]
        tapw = lambda j: dwT[:, c * CW + j:c * CW + j + 1]
        by = nextbank()
        for i_, j in enumerate(TAP_PE):
            MM(bank(by)[:, :], Dg[b][:, i_, :], win(j), i_ == 0, i_ == N_PE - 1, [("Dg", b), ("ubf", b)], [("ps", by)])
        for i_, j in enumerate(TAP_POOL):
            if i_ == 0:
                TS("gpsimd", ypool[r][:], win(j), tapw(j), None, ALU.mult, None, [("ubf", b)], [("ypool", r)])
            else:
                TS("gpsimd", ptmp[:], win(j), tapw(j), None, ALU.mult, None, [("ubf", b)], ["ptmp"])
                TT("gpsimd", ypool[r][:], ypool[r][:], ptmp[:], ALU.add, [("ypool", r), "ptmp"], [("ypool", r)])
        for i_, j in enumerate(TAP_DVE):
            if i_ == 0:
                TS("vector", ydve[r][:], win(j), tapw(j), dw_b(c), ALU.mult, ALU.add, [("ubf", b)], [("ydve", r)])
            else:
                STT("vector", ydve[r][:], win(j), tapw(j), ydve[r][:], ALU.mult, ALU.add, [("ubf", b), ("ydve", r)], [("ydve", r)])
        if N_POOL:
            TT("vector", ydve[r][:], ydve[r][:], ypool[r][:], ALU.add, [("ydve", r), ("ypool", r)], [("ydve", r)])
        TT("vector", yc[:, c, gs], bank(by)[:, :], ydve[r][:], ALU.add, [("ps", by), ("ydve", r)], [("yc", c, g)])

    b1_load(0)
    for cg in range(5):
        b1_glu(0, cg)
    GLU_SLOT = {0: (0, 1), 1: (2,), 2: (3,), 3: (4,)}
    for c in range(KC):
        if c + 1 < KC:
            b1_load(c + 1)
        for g in range(4):
            if c + 1 < KC:
                for cg in GLU_SLOT[g]:
                    b1_glu(c + 1, cg)
            b1_conv(c, g)
    if debug:
        BARRIER()
        DMA("sync", dbg["yc"], yc[:], [], [])
    BARRIER()

    o = B1_END
    ycs = sb("ycs", [128, KC, SEQ], BF16, o); o += 32768
    YCS_END = o
    assert o <= 212800, o
    o = A_LOC
    tt_t = [sb("tt%d" % i, [128, 512], F32, o + 2048 * i) for i in range(2)]; o += 4096
    sqt = [sb("sqt%d" % i, [128, 512], BF16, o + 1024 * i) for i in range(2)]; o += 2048
    ycb = [sb("ycb%d" % i, [128, 512], BF16, o + 1024 * i) for i in range(2)]; o += 2048
    mean_t = [sb("mean_t%d" % i, [128, 512], F32, o + 2048 * i) for i in range(2)]; o += 4096
    msq_t = [sb("msq_t%d" % i, [128, 512], F32, o + 2048 * i) for i in range(2)]; o += 4096
    rstd_t = [sb("rstd_t%d" % i, [128, 512], F32, o + 2048 * i) for i in range(2)]; o += 4096
    assert o <= A_END

    o = A_END + 65536
    qp = [sb("qp%d" % i, [128, SEQ], BF16, o + 4096 * i) for i in range(2)]; o += 8192
    kp = [sb("kp%d" % i, [128, L], BF16, o + 4160 * i) for i in range(2)]; o += 8320
    vp = [sb("vp%d" % i, [128, 17, 128], BF16, o + 4352 * i) for i in range(2)]; o += 8704
    wq0 = [sb("wq0_%d" % k, [128, KC, 128], BF16, o + 2048 * k) for k in range(3)]; o += 6144
    assert o <= B1_END, o
    WQ1_OFF = A_LOC + 32768 + 12288 + 8192 + 8192 + 4096
    wq1 = [sb("wq1_%d" % k, [128, KC, 128], BF16, WQ1_OFF + 2048 * k) for k in range(3)]
    assert WQ1_OFF + 6144 <= A_END + 65536
    wq = [wq0, wq1]
    OB = 6
    PB = 7

    def proj_micro(fc):
        pb = fc % 2
        ops_ = []

        def W():
            for k in range(3):
                c_lo = (2 + k) * D + 128 * fc
                DMA("gpsimd", wq[pb][k][:], wslab(w_in[:, c_lo:c_lo + 128]), [], [("wq", pb, k)])
        ops_.append(W)

        def mm_chunk(outp, lhs_fn, rhs_fn, kcs, rkey):
            def f():
                for kc in kcs:
                    MM(outp, lhs_fn(kc), rhs_fn(kc), kc == 0, kc == KC - 1, [rkey], [("ps", PB)])
            return f

        for g in range(4):
            c0, c1 = CG[g + 1]
            for kcs in ((0, 1), (2, 3), (4, 5), (6, 7)):
                ops_.append(mm_chunk(bank(PB)[:, :], lambda kc: wq[pb][0][:, kc, :], lambda kc, c0=c0, c1=c1: uT[:, kc, c0:c1], kcs, ("wq", pb, 0)))
            ops_.append(lambda g=g: TS("vector", qp[pb][:, 512 * g:512 * g + 512], bank(PB)[:, :], 0.125, None, ALU.mult, None,
                                       [("ps", PB)], [("qp", pb)]))
        for cg in range(5):
            c0, c1 = CG[cg]
            n = c1 - c0
            for kcs in ((0, 1), (2, 3), (4, 5), (6, 7)):
                ops_.append(mm_chunk(bank(PB)[:, :n], lambda kc: wq[pb][1][:, kc, :], lambda kc, c0=c0, c1=c1: uT[:, kc, c0:c1], kcs, ("wq", pb, 1)))
            ops_.append(lambda c0=c0, c1=c1, n=n: CP("vector", kp[pb][:, c0:c1], bank(PB)[:, :n], [("ps", PB)], [("kp", pb)]))
        for vg in range(5):
            tiles = [16] if vg == 4 else list(range(4 * vg, 4 * vg + 4))
            for kb in tiles:
                rows = NM if kb == 16 else 128
                for kcs in ((0, 1, 2, 3), (4, 5, 6, 7)):
                    ops_.append(mm_chunk(bank(PB)[:rows, 128 * (kb % 4):128 * (kb % 4) + 128],
                                         lambda kc, kb=kb, rows=rows: uT[:, kc, 128 * kb:128 * kb + rows],
                                         lambda kc: wq[pb][2][:, kc, :], kcs, ("wq", pb, 2)))
            if vg == 4:
                ops_.append(lambda: CP("vector", vp[pb][:NM, 16, :], bank(PB)[:NM, 0:128], [("ps", PB)], [("vp", pb)]))
            else:
                ops_.append(lambda vg=vg: CP("vector", vp[pb][:, 4 * vg:4 * vg + 4, :], bank(PB)[:, :].rearrange("p (t n) -> p t n", t=4),
                                             [("ps", PB)], [("vp", pb)]))
        return ops_

    pend0 = proj_micro(0)

    def drip0(k):
        for _ in range(k):
            if pend0:
                pend0.pop(0)()

    def B2_stats(g):
        gs = slice(512 * g, 512 * g + 512)
        q_ = g % 2
        bs, bq = 2 * q_, 2 * q_ + 1
        for c in range(KC):
            r = c % 2
            ACT(sqt[r][:], yc[:, c, gs], AF.Square, [], [("sqt", r)])
            CP("vector", ycb[r][:], yc[:, c, gs], [], [("ycb", r)])
            MM(bank(bs)[:, :], negones[:], ycb[r][:], c == 0, c == KC - 1, [("ycb", r)], [("ps", bs)])
            MM(bank(bq)[:, :], negones[:], sqt[r][:], c == 0, c == KC - 1, [("sqt", r)], [("ps", bq)])
            drip0(1)

    def B2_stats_b(g):
        q_ = g % 2
        bs, bq = 2 * q_, 2 * q_ + 1
        TS("vector", mean_t[q_][:], bank(bs)[:, :], -1.0 / D, None, ALU.mult, None, [("ps", bs)], [("mean", q_)])
        TT("vector", msq_t[q_][:], mean_t[q_][:], mean_t[q_][:], ALU.mult, [("mean", q_)], [("msq", q_)])
        STT("vector", msq_t[q_][:], bank(bq)[:, :], -1.0 / D, msq_t[q_][:], ALU.mult, ALU.subtract, [("ps", bq), ("msq", q_)], [("msq", q_)])
        ACT(rstd_t[q_][:], msq_t[q_][:], AF.Ln, [("msq", q_)], [("rstdt", q_)], bias=epsl[:, :])
        ACT(rstd_t[q_][:], rstd_t[q_][:], AF.Exp, [("rstdt", q_)], [("rstdt", q_)], scale=-0.5)

    def B2_apply(g):
        gs = slice(512 * g, 512 * g + 512)
        q_ = g % 2
        for c in range(KC):
            r = c % 2
            TT("vector", tt_t[r][:], yc[:, c, gs], mean_t[q_][:], ALU.subtract, [("mean", q_)], [("tt", r)])
            TT("vector", tt_t[r][:], tt_t[r][:], rstd_t[q_][:], ALU.mult, [("tt", r), ("rstdt", q_)], [("tt", r)])
            ACT(ycs[:, c, gs], tt_t[r][:], AF.Silu, [("tt", r)], [("ycs", c, g)], scale=ln_g(c), bias=ln_b(c))
            drip0(1)

    B2_stats(0)
    B2_stats_b(0)
    for g in range(4):
        if g < 3:
            B2_stats(g + 1)
        B2_apply(g)
        if g < 3:
            B2_stats_b(g + 1)
    while pend0:
        pend0.pop(0)()
    if debug:
        BARRIER()
        DMA("sync", dbg["ycs"], ycs[:], [], [])
    BARRIER()

    o = A_LOC
    yaT = sb("yaT", [128, KC, SEQ], BF16, o); o += 32768
    YA_END = o
    et = [sb("et%d" % i, [128, 2, 512], F32, o + 4096 * i) for i in range(3)]; o += 12288
    spt = [sb("spt%d" % i, [128, 2, 512], BF16, o + 2048 * i) for i in range(4)]; o += 8192
    at = [sb("at%d" % i, [128, 2, 512], BF16, o + 2048 * i) for i in range(4)]; o += 8192
    St = [sb("St%d" % i, [128, 2, 512], BF16, o + 2048 * i) for i in range(2)]; o += 4096
    wq1_off = o; o += 6144
    assert o <= B1_END, o

    def ppv(i):
        return pp[i][:, :].rearrange("p (h n) -> p h n", h=2)

    blocks = []
    for pr_ in range(NH // 2):
        for g in range(4):
            lst = []
            for j in (4, 3, 2, 1, 0):
                lst.append(dict(kb=4 * g + j, np=(NM if j == 4 else 128), cs=(0 if j == 0 else 128 * j - 16), j=j))
            for kb in range(4 * g - 1, -1, -1):
                lst.append(dict(kb=kb, np=128, cs=0, j=None))
            for i, bl in enumerate(lst):
                bl.update(fc=pr_, g=g, first=(i == 0), last=(i == len(lst) - 1), hg=pr_ * 4 + g)
                blocks.append(bl)
    for n, bl in enumerate(blocks):
        bl["n"] = n

    def S1(bl):
        n, fc, g, kb, np_, cs = bl["n"], bl["fc"], bl["g"], bl["kb"], bl["np"], bl["cs"]
        pb = fc % 2
        z = n % 3
        part = bl["j"] is not None
        for hh in range(2):
            hp = hh * 64
            MM(ppv(z)[:np_, hh, cs:512], kp[pb][hp:hp + 64, 128 * kb:128 * kb + np_], qp[pb][hp:hp + 64, 512 * g + cs:512 * g + 512],
               True, not part, [("kp", pb), ("qp", pb)], [("pp", z)])
        if part:
            for hh in range(2):
                MM(ppv(z)[:np_, hh, cs:512], ident[:np_, :np_], MB[:np_, bl["j"], cs:512], False, True, [], [("pp", z)])

    def S2(bl):
        n, np_, cs = bl["n"], bl["np"], bl["cs"]
        z = n % 3
        ACT(et[n % 3][:np_, :, cs:512], ppv(z)[:np_, :, cs:512], AF.Exp, [("pp", z)], [("et", n % 3)])
        ACT(spt[n % 4][:np_, :, cs:512], et[n % 3][:np_, :, cs:512], AF.Ln, [("et", n % 3)], [("spt", n % 4)], bias=1.0)

    def S3(bl):
        n, np_, cs = bl["n"], bl["np"], bl["cs"]
        z = n % 3
        par = bl["hg"] % 2
        for hh in range(2):
            MM(ppv(z)[:np_, hh, cs:512], negtri[:np_, :np_], spt[n % 4][:np_, hh, cs:512], False, bl["first"],
               [("spt", n % 4)], [("pp", z)], skip=True)
        if not bl["first"]:
            for hh in range(2):
                MM(ppv(z)[:np_, hh, cs:512], negones[:, :np_], St[par][:, hh, cs:512], False, True, [("S", par)], [("pp", z)], skip=True)

    def S4(bl):
        n, np_, cs = bl["n"], bl["np"], bl["cs"]
        par = bl["hg"] % 2
        if bl["first"]:
            MEMSET("gpsimd", St[par][:], 0.0, [], [("S", par)])
        if not bl["last"]:
            TT("vector", St[par][:np_, :, cs:512], St[par][:np_, :, cs:512], spt[n % 4][:np_, :, cs:512], ALU.add,
               [("S", par), ("spt", n % 4)], [("S", par)])

    def S5(bl):
        n, np_, cs = bl["n"], bl["np"], bl["cs"]
        z = n % 3
        ACT(at[n % 4][:np_, :, cs:512], ppv(z)[:np_, :, cs:512], AF.Exp, [("pp", z)], [("at", n % 4)])

    def S6(bl):
        n, fc, g, kb, np_, cs = bl["n"], bl["fc"], bl["g"], bl["kb"], bl["np"], bl["cs"]
        pb = fc % 2
        if bl["first"]:
            MM(bank(OB)[:, :], zeros[:, :], MB[:, 0, :], True, False, [], [("ps", OB)])
        for hh in range(2):
            hp = hh * 64
            MM(bank(OB)[hp:hp + 64, cs:512], vp[pb][:np_, kb, hp:hp + 64], at[n % 4][:np_, hh, cs:512], False, bl["last"],
               [("at", n % 4), ("vp", pb)], [("ps", OB)])
        if bl["last"]:
            CP("vector", yaT[:, fc, 512 * g:512 * g + 512], bank(OB)[:, :], [("ps", OB)], [("yaT", fc, g)])

    castjobs = []
    for s_ in range(2):
        castjobs.append((slabview(wo_bf, s_), wslab(w_o[:, 512 * s_:512 * s_ + 512]), ("wo_bf", s_)))
    for s_ in range(FC // 2):
        castjobs.append((slabview(wffin_bf, s_)[:, :, 0:256], wslab(w_ffn_in[:, 256 * s_:256 * s_ + 256]), ("wffin_bf", s_)))
        castjobs.append((slabview(wffin_bf, s_)[:, :, 256:512], wslab(w_ffn_in[:, DFF + 256 * s_:DFF + 256 * s_ + 256]), ("wffin_bf", s_)))
    MIX_OFF = YA_END
    o = MIX_OFF + 32768
    wsb = [[sb("wsb%d_%d" % (i, k), [128, KC, 128], BF16, o + 2048 * (4 * i + k)) for k in range(4)] for i in range(2)]
    assert o + 2048 * 4 >= WQ1_OFF + 6144

    def b5_load(f, b):
        fs_ = slice(f * 128, (f + 1) * 128)
        DMA("gpsimd", wsb[b][0][:], wslab(w_in[:, 5 * D + f * 128:5 * D + (f + 1) * 128]), [], [("wsb", b, 0)])
        DMA("gpsimd", wsb[b][1][:], wslab(w_in[:, 6 * D + f * 128:6 * D + (f + 1) * 128]), [], [("wsb", b, 1)])
        DMA("gpsimd", wsb[b][2][:], wslab(w_conv_out[:, fs_]), [], [("wsb", b, 2)])
        DMA("gpsimd", wsb[b][3][:], wslab(w_attn_out[:, fs_]), [], [("wsb", b, 3)])

    NB = len(blocks)
    pend = []
    for step in range(NB + 2):
        if step < NB:
            bl = blocks[step]
            if bl["first"] and bl["g"] == 0:
                for _ in range(3):
                    if castjobs:
                        dst_, src_, key_ = castjobs.pop(0)
                        DMA("gpsimd", dst_, src_, [], [key_])
            if bl["first"] and bl["g"] == 0 and bl["fc"] == NH // 2 - 1:
                b5_load(0, 1)
            if bl["first"] and bl["g"] == 0 and bl["fc"] + 1 < NH // 2:
                while pend:
                    pend.pop(0)()
                pend = proj_micro(bl["fc"] + 1)
            S1(bl)
            S2(bl)
        if 0 <= step - 1 < NB:
            S3(blocks[step - 1])
            S4(blocks[step - 1])
            S5(blocks[step - 1])
        if 0 <= step - 2 < NB:
            S6(blocks[step - 2])
        big = step < NB and (512 - blocks[step]["cs"]) >= 400
        for _ in range(3 if big else 0):
            if pend:
                pend.pop(0)()
    while pend:
        pend.pop(0)()
    assert not castjobs
    if debug:
        BARRIER()
        DMA("sync", dbg["ya"], yaT[:], [], [])
    BARRIER()

    o = YA_END
    mixT = sb("mixT", [128, KC, SEQ], BF16, o); o += 32768
    o += 16384
    sg = [[sb("sg%d_%d" % (i, k), [128, 512], F32, o + 2048 * (2 * i + k)) for k in range(2)] for i in range(2)]; o += 8192
    t1 = [sb("t1_%d" % i, [128, 512], F32, o + 2048 * i) for i in range(2)]; o += 4096
    assert o <= B1_END
    rr = [0]
    for f in range(KC):
        b = (f + 1) % 2
        fs = slice(f * 128, (f + 1) * 128)
        if f > 0:
            b5_load(f, b)
        for g in range(4):
            gs = slice(512 * g, 512 * g + 512)
            us = slice(NM + 512 * g, NM + 512 * g + 512)
            r = rr[0] % 2
            rr[0] += 1
            bks = [4 * r + k for k in range(4)]
            srcs = [uT[:, :, us], uT[:, :, us], ycs[:, :, gs], yaT[:, :, gs]]
            for k in range(4):
                for kc in range(KC):
                    MM(bank(bks[k])[:, :], wsb[b][k][:, kc, :], srcs[k][:, kc, :], kc == 0, kc == KC - 1, [("wsb", b, k)], [("ps", bks[k])])
            ACT(sg[r][0][:], bank(bks[0])[:, :], AF.Sigmoid, [("ps", bks[0])], [("sg", r, 0)], bias=gate_b(f))
            ACT(sg[r][1][:], bank(bks[1])[:, :], AF.Sigmoid, [("ps", bks[1])], [("sg", r, 1)], bias=gate_b(8 + f))
            TT("vector", sg[r][0][:], bank(bks[2])[:, :], sg[r][0][:], ALU.mult, [("ps", bks[2]), ("sg", r, 0)], [("sg", r, 0)])
            TT("vector", sg[r][1][:], bank(bks[3])[:, :], sg[r][1][:], ALU.mult, [("ps", bks[3]), ("sg", r, 1)], [("sg", r, 1)])
            TT("vector", mixT[:, f, gs], sg[r][0][:], sg[r][1][:], ALU.add, [("sg", r, 0), ("sg", r, 1)], [("mixT", f, g)])
    B5_END = o
    o = B5_END
    wsl5 = [sb("wsl5_%d" % i, [128, KC, 512], BF16, o + 8192 * i) for i in range(2)]; o += 16384
    gbcA = sb("gbcA", [128, D], F32, o); o += 4096
    gbcB = sb("gbcB", [128, D], F32, o); o += 4096
    gbcC = sb("gbcC", [128, D], F32, o); o += 4096
    u2T0 = sb("u2T0", [128, KC, 512], BF16, o); o += 8192
    assert o <= B1_END, o
    DMA("sync", gbcA[:], g_postmix.partition_broadcast(128), [], ["gbcA"])
    DMA("sync", gbcB[:], g_preffn.partition_broadcast(128), [], ["gbcB"])
    DMA("sync", gbcC[:], g_postffn.partition_broadcast(128), [], ["gbcC"])
    pre_wo = {}
    for s_ in range(2):
        DMA("gpsimd", wsl5[s_][:], slabview(wo_bf, s_), [("wo_bf", s_)], [("wsl", s_)])
        pre_wo[(0, 0, s_)] = s_
    if debug:
        BARRIER()
        DMA("sync", dbg["mix"], mixT[:], [], [])
    BARRIER()

    o = MIX_OFF + 32768
    hT = sb("hT", [128, FC, 512], BF16, o); o += 22528
    sqj5 = sb("sqj5", [128, D], BF16, o); o += 2048
    sl = [sb("sl%d" % i, [128, 512], F32, o + 2048 * i) for i in range(2)]; o += 4096
    assert o <= B5_END, o
    o = B1_END
    h1 = [sb("h1_%d" % i, [128, 4, D], F32, o + 16384 * i) for i in range(2)]; o += 32768
    assert o <= 212800, o
    o = PERS
    wfo = sb("wfo", [128, FC, D], BF16, o); o += 45056
    tmpA = [sb("tmpA%d" % i, [128, D], F32, o + 4096 * i) for i in range(2)]; o += 8192
    u2b = [sb("u2b%d" % i, [128, D], BF16, o + 2048 * i) for i in range(2)]; o += 4096
    u2T1 = sb("u2T1", [128, KC, 512], BF16, o); o += 8192
    assert o <= MIX_OFF, o
    u2T = [u2T0, u2T1]

    def load_wfo(k0, k1):
        DMA("gpsimd", wfo[:, k0:k1, :], w_ffn_out[128 * k0:128 * k1, :].rearrange("(kc p) f -> p kc f", p=128), [], ["wfo"])

    cnt5 = dict(t=0, a=0, w=2)

    def rms_rstd(src_ap, c_, reads):
        ACT(sqj5[:], src_ap, AF.Square, reads, ["sqj5", ("st", c_)], accum_out=st[:, c_:c_ + 1])
        ACT(st[:, 8 + c_:9 + c_], st[:, c_:c_ + 1], AF.Ln, [("st", c_)], [("st", 8 + c_)], scale=1.0 / D, bias=epsr[:, :])
        ACT(st[:, 16 + c_:17 + c_], st[:, 8 + c_:9 + c_], AF.Exp, [("st", 8 + c_)], [("st", 16 + c_)], scale=-0.5)

    def C_pieces(g):
        pg = g % 2
        pieces = []
        for half in range(2):
            tiles = (2 * half, 2 * half + 1)

            def Sstep(s_, tiles=tiles, half=half):
                if (g, half, s_) in pre_wo:
                    b = pre_wo[(g, half, s_)]
                else:
                    b = cnt5["w"] % 2
                    cnt5["w"] += 1
                    DMA("gpsimd", wsl5[b][:], slabview(wo_bf, s_), [("wo_bf", s_)], [("wsl", b)])
                for tt in tiles:
                    pi = 2 + tt % 2
                    for kc in range(KC):
                        MM(pp[pi][:, 512 * s_:512 * s_ + 512], mixT[:, kc, 512 * g + 128 * tt:512 * g + 128 * tt + 128],
                           wsl5[b][:, kc, :], kc == 0, kc == KC - 1, [("wsl", b)], [("pp", pi)])

            def P1(tt):
                tk = 4 * g + tt
                i = tt % 2
                pi = 2 + i
                a = cnt5["a"] % 2
                cnt5["a"] += 1
                DMA("sync", h1[pg][:, tt, :], x[128 * tk:128 * tk + 128, :], [], [("h1", pg, tt)])
                c_ = 3 * i
                rms_rstd(pp[pi][:, :], c_, [("pp", pi)])
                STT("vector", tmpA[a][:], pp[pi][:, :], st[:, 16 + c_:17 + c_], gbcA[:], ALU.mult, ALU.mult,
                    [("pp", pi), ("st", 16 + c_), "gbcA"], [("tmpA", a)])
                TT("vector", h1[pg][:, tt, :], h1[pg][:, tt, :], tmpA[a][:], ALU.add, [("tmpA", a), ("h1", pg, tt)], [("h1", pg, tt)])

            def P2(tt):
                i = tt % 2
                c_ = 3 * i + 1
                rms_rstd(h1[pg][:, tt, :], c_, [("h1", pg, tt)])
                STT("vector", u2b[i][:], h1[pg][:, tt, :], st[:, 16 + c_:17 + c_], gbcB[:], ALU.mult, ALU.mult,
                    [("h1", pg, tt), ("st", 16 + c_), "gbcB"], [("u2b", i)])

            def P3(tt):
                i = tt % 2
                pi = 2 + i
                trv = pp[pi][:, 0:512].bitcast(BF16).rearrange("p (c n) -> p c n", c=8)
                for c in range(KC):
                    TR(trv[:, c, :], u2b[i][:, c * 128:(c + 1) * 128], ident[:], [("u2b", i)], [("pp", pi)])
                CP(evac_eng(), u2T[pg][:, :, 128 * tt:128 * tt + 128], trv[:, :, :], [("pp", pi)], [("u2T", pg, tt)])

            pieces.append(lambda Sstep=Sstep: Sstep(0))
            pieces.append(lambda Sstep=Sstep: Sstep(1))
            for fn in (P1, P2, P3):
                for tt in tiles:
                    pieces.append(lambda fn=fn, tt=tt: fn(tt))
        return pieces

    def D1_slab(g, s_):
        pg = g % 2
        allu2 = [("u2T", pg, t_) for t_ in range(4)]
        b = cnt5["w"] % 2
        cnt5["w"] += 1
        DMA("gpsimd", wsl5[b][:], slabview(wffin_bf, s_), [("wffin_bf", s_)], [("wsl", b)])
        for fi in range(2):
            hc = 2 * s_ + fi
            r = hc % 2
            ba, bb = 2 * r, 2 * r + 1
            for kc in range(KC):
                MM(bank(ba)[:, :], wsl5[b][:, kc, fi * 128:(fi + 1) * 128], u2T[pg][:, kc, :], kc == 0, kc == KC - 1,
                   [("wsl", b)] + allu2, [("ps", ba)])
            for kc in range(KC):
                MM(bank(bb)[:, :], wsl5[b][:, kc, 256 + fi * 128:256 + (fi + 1) * 128], u2T[pg][:, kc, :], kc == 0, kc == KC - 1,
                   [("wsl", b)] + allu2, [("ps", bb)])
            ACT(sl[r][:], bank(ba)[:, :], AF.Silu, [("ps", ba)], [("sl", r)])
            TT("vector", hT[:, hc, :], bank(bb)[:, :], sl[r][:], ALU.mult, [("ps", bb), ("sl", r)], [("hT", hc)])

    def D2E_tile(g, tt):
        pg = g % 2
        tk = 4 * g + tt
        allh = [("hT", hc) for hc in range(FC)]
        i = tt % 2
        pi = 2 + i
        a = cnt5["a"] % 2
        cnt5["a"] += 1
        for hc in range(FC):
            for s_ in range(2):
                MM(pp[pi][:, 512 * s_:512 * s_ + 512], hT[:, hc, 128 * tt:128 * tt + 128], wfo[:, hc, 512 * s_:512 * s_ + 512],
                   hc == 0, hc == FC - 1, ["wfo"] + allh, [("pp", pi)])
        c_ = 3 * i + 2
        rms_rstd(pp[pi][:, :], c_, [("pp", pi)])
        STT("vector", tmpA[a][:], pp[pi][:, :], st[:, 16 + c_:17 + c_], gbcC[:], ALU.mult, ALU.mult,
            [("pp", pi), ("st", 16 + c_), "gbcC"], [("tmpA", a)])
        TT("vector", tmpA[a][:], tmpA[a][:], h1[pg][:, tt, :], ALU.add, [("tmpA", a), ("h1", pg, tt)], [("tmpA", a)])
        DMA("sync", out[128 * tk:128 * tk + 128, :], tmpA[a][:], [("tmpA", a)], [])

    for pc in C_pieces(0):
        pc()
    for g in range(4):
        nxt = C_pieces(g + 1) if g < 3 else []
        for s_ in range(FC // 2):
            D1_slab(g, s_)
            if g == 0:
                load_wfo(2 * s_, 2 * s_ + 2)
            take = 2 if s_ < 6 else 1
            for _ in range(take):
                if nxt:
                    nxt.pop(0)()
        for tt in range(4):
            D2E_tile(g, tt)
            if nxt:
                nxt.pop(0)()
        while nxt:
            nxt.pop(0)()

    from contextlib import ExitStack
    with ExitStack() as es:
        csem = {e: es.enter_context(nc.semaphore("cs_" + e)) for e in ENGS}
        dsems = {"sync": [es.enter_context(nc.semaphore("ds%d" % i)) for i in range(8)],
                 "gpsimd": [es.enter_context(nc.semaphore("dg%d" % i)) for i in range(8)]}
        block = es.enter_context(nc.Block())
        run = P.emit(csem, dsems)

        @block.sync
        def _(e):
            run("sync", e)

        @block.scalar
        def _(e):
            run("scalar", e)

        @block.vector
        def _(e):
            run("vector", e)

        @block.gpsimd
        def _(e):
            run("gpsimd", e)

        @block.tensor
        def _(e):
            run("tensor", e)
    return nc, P


def make_in_maps(inputs):
    f = lambda a: np.ascontiguousarray(np.asarray(a, dtype=np.float32))
    x = f(inputs["x"])
    col = lambda v, n: f(v).reshape(n, 128).T
    pvec = np.ascontiguousarray(np.concatenate([
        col(inputs["gate_b"][0], 16), col(inputs["dw_b"][0], 8),
        col(inputs["conv_ln_g"][0], 8), col(inputs["conv_ln_b"][0], 8)], axis=1))
    dw = f(inputs["dw_w"][0])
    dwT = np.ascontiguousarray(dw.T.reshape(KC, 128, CW).transpose(1, 0, 2).reshape(128, KC * CW))
    shared = {
        "meta": f(inputs["meta_tokens"]),
        "w_in": f(inputs["w_in"][0]),
        "w_conv_out": f(inputs["w_conv_out"][0]),
        "w_attn_out": f(inputs["w_attn_out"][0]),
        "w_o": f(inputs["w_o"][0]),
        "w_ffn_in": f(inputs["w_ffn_in"][0]),
        "w_ffn_out": f(inputs["w_ffn_out"][0]),
        "g_pre": f(inputs["pre_mix_g"][0]),
        "g_postmix": f(inputs["post_mix_g"][0]),
        "g_preffn": f(inputs["pre_ffn_g"][0]),
        "g_postffn": f(inputs["post_ffn_g"][0]),
        "pvec": pvec,
        "dwT": dwT,
    }
    return [dict(shared, x=np.ascontiguousarray(x[b])) for b in range(x.shape[0])]


def kernel(**inputs):
    in_maps = make_in_maps(inputs)
    nc, _ = build_program(debug=False)
    res = run_bass_kernel_spmd(nc, in_maps, core_ids=list(range(NCORES)))
    return np.stack([np.asarray(r["out"], dtype=np.float32) for r in res.results], axis=0)
```

```python
import numpy as np
import concourse.bass as bass
import concourse.mybir as mybir
from concourse.bass_utils import run_bass_kernel_spmd

F32 = mybir.dt.float32
BF16 = mybir.dt.bfloat16
AF = mybir.ActivationFunctionType
ALU = mybir.AluOpType

D = 1024
SEQ = 2048
NM = 16
L = SEQ + NM
NH = 16
DH = 64
DFF = 2816
KC = D // 128
FC = DFF // 128
CW = 31
RMS_EPS = 1e-6
LN_EPS = 1e-5
NCORES = 8
ENGS = ("sync", "scalar", "vector", "gpsimd", "tensor")


class Prog:
    def __init__(self):
        self.ops = []

    def add(self, eng, fn, reads=(), writes=(), dma=False):
        self.ops.append((eng, fn, tuple(reads) + ("BAR",), tuple(writes), dma))

    def barrier(self, fn):
        self.ops.append(("gpsimd", fn, (), ("BAR",), False))

    def analyze(self):
        ops = self.ops
        last_w = {}
        readers = {}
        pos = []
        cnt = {e: 0 for e in ENGS}
        for (eng, _, _, _, _) in ops:
            pos.append(cnt[eng])
            cnt[eng] += 1
        final = []
        needed = set()
        for i, (eng, fn, reads, writes, dma) in enumerate(ops):
            deps = set()
            raw = set()
            for r in reads:
                if r in last_w:
                    deps.add(last_w[r])
                    raw.add(last_w[r])
            for w in writes:
                if w in last_w:
                    deps.add(last_w[w])
                deps.update(readers.get(w, ()))
            deps.discard(i)
            for r in reads:
                readers.setdefault(r, []).append(i)
            for w in writes:
                last_w[w] = i
                readers[w] = []
            best = {}
            keep = set()
            for d in deps:
                deng, _, _, _, ddma = ops[d]
                if ddma:
                    keep.add(d)
                    continue
                if deng == eng and not dma:
                    if d not in raw or eng == "tensor":
                        continue
                if deng not in best or pos[best[deng]] < pos[d]:
                    best[deng] = d
            keep.update(best.values())
            final.append(keep)
            needed |= keep
        self.final = final
        self.needed = needed

    def emit(self, csem, dsems):
        self.analyze()
        ops = self.ops
        sem_of = {}
        ccount = {e: 0 for e in ENGS}
        dcount = {e: [0] * len(dsems[e]) for e in dsems}
        dk = {e: 0 for e in dsems}
        dma_prev = {}
        for i, (eng, fn, reads, writes, dma) in enumerate(ops):
            if dma:
                pool = dsems[eng]
                s = dk[eng] % len(pool)
                dk[eng] += 1
                dma_prev[i] = (pool[s], dcount[eng][s])
                dcount[eng][s] += 16
                sem_of[i] = (pool[s], dcount[eng][s])
            elif i in self.needed:
                ccount[eng] += 1
                sem_of[i] = (csem[eng], ccount[eng])
        streams = {e: [] for e in ENGS}
        for i, op in enumerate(ops):
            streams[op[0]].append(i)
        self.stats = {e: len(streams[e]) for e in ENGS}

        def run(engname, eng):
            wd = {}
            for i in streams[engname]:
                _, fn, _, _, dma = ops[i]
                waits = {}
                for d in self.final[i]:
                    s, v = sem_of[d]
                    k = id(s)
                    if k not in waits or waits[k][1] < v:
                        waits[k] = (s, v)
                if dma:
                    s, v = dma_prev[i]
                    if v > 0:
                        k = id(s)
                        if k not in waits or waits[k][1] < v:
                            waits[k] = (s, v)
                for k, (s, v) in waits.items():
                    if wd.get(k, 0) >= v:
                        continue
                    wd[k] = v
                    eng.wait_ge(s, v)
                ins = fn(eng)
                if i in sem_of:
                    ins.then_inc(sem_of[i][0], 16 if dma else 1)
            if engname == "sync":
                for e in dsems:
                    for s, c in zip(dsems[e], dcount[e]):
                        if c > 0:
                            eng.wait_ge(s, c)
        return run


def build_program(debug=False):
    nc = bass.Bass("TRN2", target_bir_lowering=False)

    def din(name, shape, dt=F32):
        return nc.dram_tensor(name, list(shape), dt, kind="ExternalInput").ap()

    x = din("x", [SEQ, D])
    meta = din("meta", [NM, D])
    w_in = din("w_in", [D, 7 * D])
    w_conv_out = din("w_conv_out", [D, D])
    w_attn_out = din("w_attn_out", [D, D])
    w_o = din("w_o", [D, D])
    w_ffn_in = din("w_ffn_in", [D, 2 * DFF])
    w_ffn_out = din("w_ffn_out", [DFF, D])
    g_pre = din("g_pre", [D])
    g_postmix = din("g_postmix", [D])
    g_preffn = din("g_preffn", [D])
    g_postffn = din("g_postffn", [D])
    pvec_d = din("pvec", [128, 40])
    dwT_d = din("dwT", [128, KC * CW])
    out = nc.dram_tensor("out", [SEQ, D], F32, kind="ExternalOutput").ap()
    wffin_bf = nc.dram_tensor("wffin_bf", [FC // 2, 128, KC * 512], BF16, kind="Internal").ap()
    wo_bf = nc.dram_tensor("wo_bf", [2, 128, KC * 512], BF16, kind="Internal").ap()

    def slabview(t3, i):
        return t3[i].rearrange("p (kc j) -> p kc j", kc=KC)
    dbg = {}
    if debug:
        for nm in ("ya", "ycs", "mix"):
            dbg[nm] = nc.dram_tensor("dbg_" + nm, [128, KC, SEQ], BF16, kind="ExternalOutput").ap()
        dbg["uT"] = nc.dram_tensor("dbg_uT", [128, KC, L], BF16, kind="ExternalOutput").ap()
        dbg["q"] = nc.dram_tensor("dbg_q", [128, KC, SEQ], BF16, kind="ExternalOutput").ap()
        dbg["k"] = nc.dram_tensor("dbg_k", [128, KC, L], BF16, kind="ExternalOutput").ap()
        dbg["v"] = nc.dram_tensor("dbg_v", [128, 17, D], BF16, kind="ExternalOutput").ap()
        dbg["yc"] = nc.dram_tensor("dbg_yc", [128, KC, SEQ], F32, kind="ExternalOutput").ap()

    P = Prog()

    SB_BASE = 16512

    def sb(name, shape, dt, off):
        assert off + SB_BASE < 229376
        return nc.alloc_sbuf_tensor_at(name, list(shape), dt, offset=off + SB_BASE)

    o = 0
    ident = sb("ident", [128, 128], BF16, o); o += 256
    negtri = sb("negtri", [128, 128], BF16, o); o += 256
    negones = sb("negones", [128, 128], BF16, o); o += 256
    zeros = sb("zeros", [128, 128], BF16, o); o += 256
    onesf = sb("onesf", [128, 128], F32, o); o += 512
    MB = sb("MB", [128, 5, 512], BF16, o); o += 5120
    pvec = sb("pvec", [128, 40], F32, o); o += 160
    dwT = sb("dwT", [128, KC * CW], F32, o); o += 992
    ss = sb("ss", [128, 32], F32, o); o += 128
    lnt = sb("lnt", [128, 32], F32, o); o += 128
    rstd = sb("rstd", [128, 32], F32, o); o += 128
    st = sb("st", [128, 64], F32, o); o += 256
    epsr = sb("epsr", [128, 1], F32, o); o += 32
    epsl = sb("epsl", [128, 1], F32, o); o += 32
    barj = sb("barj", [128, 8], F32, o); o += 32
    assert o <= 9216
    PERS = 9216

    pp = [nc.alloc_psum_tensor("pp%d" % i, [128, 1024], F32) for i in range(4)]

    def bank(i):
        return pp[i // 2][:, (i % 2) * 512:(i % 2) * 512 + 512]

    def bank_bf(i):
        return pp[i // 2][:, (i % 2) * 512:(i % 2) * 512 + 512].bitcast(BF16).rearrange("p (c n) -> p c n", c=8)

    def MM(outp, lhsT, rhs, start, stop, reads, writes, skip=False):
        P.add("tensor", lambda e: e.matmul(outp, lhsT, rhs, start=start, stop=stop, skip_group_check=skip), reads, writes)

    def TR(outp, in_, idn, reads, writes):
        P.add("tensor", lambda e: e.transpose(outp, in_, idn), reads, writes)

    def ACT(outp, in_, func, reads, writes, **kw):
        P.add("scalar", lambda e: e.activation(out=outp, in_=in_, func=func, **kw), reads, writes)

    def TT(eng, outp, in0, in1, op, reads, writes):
        P.add(eng, lambda e: e.tensor_tensor(out=outp, in0=in0, in1=in1, op=op), reads, writes)

    def TS(eng, outp, in0, s1, s2, op0, op1, reads, writes):
        if s2 is None:
            P.add(eng, lambda e: e.tensor_scalar(out=outp, in0=in0, scalar1=s1, scalar2=None, op0=op0), reads, writes)
        else:
            P.add(eng, lambda e: e.tensor_scalar(out=outp, in0=in0, scalar1=s1, scalar2=s2, op0=op0, op1=op1), reads, writes)

    def STT(eng, outp, in0, scalar, in1, op0, op1, reads, writes):
        P.add(eng, lambda e: e.scalar_tensor_tensor(out=outp, in0=in0, scalar=scalar, in1=in1, op0=op0, op1=op1), reads, writes)

    def CP(eng, outp, in_, reads, writes):
        if eng == "scalar":
            ACT(outp, in_, AF.Copy, reads, writes)
        else:
            P.add(eng, lambda e: e.tensor_copy(out=outp, in_=in_), reads, writes)

    def DMA(eng, outp, in_, reads, writes):
        P.add(eng, lambda e: e.dma_start(out=outp, in_=in_), reads, writes, dma=True)

    def MEMSET(eng, ap, val, reads, writes):
        P.add(eng, lambda e: e.memset(ap, val), reads, writes)

    def ASEL(outp, in_, pattern, cmp, fill, base, cm, reads, writes):
        P.add("gpsimd", lambda e: e.affine_select(out=outp, in_=in_, pattern=pattern, compare_op=cmp, fill=fill,
                                                 base=base, channel_multiplier=cm), reads, writes)

    def BARRIER():
        P.barrier(lambda e: e.memset(barj[:], 0.0))

    def wslab(src2d):
        return src2d.rearrange("(kc p) f -> p kc f", p=128)

    MEMSET("gpsimd", ident[:], 1.0, [], ["c_ident"])
    ASEL(ident[:], ident[:], [[-1, 128]], ALU.is_equal, 0.0, 0, 1, ["c_ident"], ["c_ident"])
    MEMSET("gpsimd", epsr[:], RMS_EPS, [], ["c_eps"])
    MEMSET("gpsimd", epsl[:], LN_EPS, [], ["c_eps2"])
    MEMSET("gpsimd", ss[:], 0.0, [], ["c_ss"])
    MEMSET("gpsimd", st[:], 0.0, [], ["c_st"])
    DMA("sync", pvec[:], pvec_d, [], ["c_pvec"])
    DMA("sync", dwT[:], dwT_d, [], ["c_dwT"])
    BARRIER()

    gate_b = lambda f: pvec[:, f:f + 1]
    dw_b = lambda c: pvec[:, 16 + c:17 + c]
    ln_g = lambda c: pvec[:, 24 + c:25 + c]
    ln_b = lambda c: pvec[:, 32 + c:33 + c]

    def tile_rows(t):
        return NM if t == 0 else 128

    def tile_cols(t):
        return (0, NM) if t == 0 else (NM + 128 * (t - 1), NM + 128 * t)

    CG = [(0, NM)] + [(NM + 512 * i, NM + 512 * (i + 1)) for i in range(4)]

    evac_rr = [0]

    def evac_eng():
        evac_rr[0] += 1
        return "scalar" if evac_rr[0] % 2 else "vector"

    def phase_A(first, gbc, xt, xn, sqj, uT, trbanks):
        DMA("sync", gbc[:], g_pre.partition_broadcast(128), [], ["gbc"])

        def stage1(t):
            rows = tile_rows(t)
            b = t % len(xt)
            src = meta if t == 0 else x[128 * (t - 1):128 * t, :]
            DMA("sync", xt[b][:rows, :], src, [], [("xt", b)])
            if first:
                ACT(sqj[:rows, :], xt[b][:rows, :], AF.Square, [("xt", b)], ["sqj", ("ss", t)], accum_out=ss[:rows, t:t + 1])
                ACT(lnt[:rows, t:t + 1], ss[:rows, t:t + 1], AF.Ln, [("ss", t)], [("lnt", t)], scale=1.0 / D, bias=epsr[:rows, :])
                ACT(rstd[:rows, t:t + 1], lnt[:rows, t:t + 1], AF.Exp, [("lnt", t)], [("rstd", t)], scale=-0.5)
            STT("vector", xn[b][:rows, :], xt[b][:rows, :], rstd[:rows, t:t + 1], gbc[:rows, :], ALU.mult, ALU.mult,
                [("xt", b), ("rstd", t), "gbc"], [("xn", b)])

        def stage2(t):
            rows = tile_rows(t)
            c0, c1 = tile_cols(t)
            b = t % len(xt)
            tb = trbanks[t % len(trbanks)]
            for c in range(KC):
                TR(bank_bf(tb)[:, c, :rows], xn[b][:rows, c * 128:(c + 1) * 128], ident[:rows, :rows], [("xn", b)], [("ps", tb)])
            CP("vector", uT[:, :, c0:c1], bank_bf(tb)[:, :, :rows], [("ps", tb)], [("uT", t)])

        stage1(0)
        stage1(1)
        for t in range(17):
            if t + 2 < 17:
                stage1(t + 2)
            stage2(t)

    uT = sb("uT", [128, KC, L], BF16, PERS)
    o = PERS + 33024
    A_LOC = o
    gbc1 = sb("gbc1", [128, D], F32, o); o += 4096
    xt1 = [sb("xt1_%d" % i, [128, D], F32, o + 4096 * i) for i in range(4)]; o += 16384
    xn1 = [sb("xn1_%d" % i, [128, D], BF16, o + 2048 * i) for i in range(4)]; o += 8192
    sqj1 = sb("sqj1", [128, D], BF16, o); o += 2048
    A_END = o

    phase_A(True, gbc1, xt1, xn1, sqj1, uT, [6, 7])
    MEMSET("gpsimd", negtri[:], -1.0, [], ["c_negtri"])
    ASEL(negtri[:], negtri[:], [[-1, 128]], ALU.is_ge, 0.0, 0, 1, ["c_negtri"], ["c_negtri"])
    MEMSET("gpsimd", negones[:], -1.0, [], ["c_negones"])
    MEMSET("gpsimd", zeros[:], 0.0, [], ["c_zeros"])
    MEMSET("gpsimd", onesf[:], 1.0, [], ["c_onesf"])
    for j in range(5):
        MEMSET("gpsimd", MB[:, j, :], 0.0, [], [("c_MB", j)])
        ASEL(MB[:, j, :], MB[:, j, :], [[1, 512]], ALU.is_gt, -30000.0, 16 - 128 * j, -1, [("c_MB", j)], [("c_MB", j)])
    if debug:
        DMA("sync", dbg["uT"], uT[:], [("uT", t) for t in range(17)], [])
    allu = [("uT", t) for t in range(17)]

    def cgkeys(cg):
        return [("uT", 0)] if cg == 0 else [("uT", t) for t in range(4 * (cg - 1) + 1, 4 * cg + 1)]
    pr = [0]

    def nextbank(n=6):
        pr[0] = (pr[0] + 1) % n
        return pr[0]

    N_PE, N_DVE, N_POOL = 23, 8, 0
    assert N_PE + N_DVE + N_POOL == CW
    TAP_PE = list(range(0, N_PE))
    TAP_DVE = list(range(N_PE, N_PE + N_DVE))
    TAP_POOL = list(range(N_PE + N_DVE, CW))
    o = A_END
    yc = sb("yc", [128, KC, SEQ], F32, o); o += 65536
    UB = 14 + L
    ubf = [sb("ubf%d" % i, [128, UB], BF16, o + 4160 * i) for i in range(2)]; o += 8320
    Dg = [sb("Dg%d" % i, [128, N_PE, 128], BF16, o + 256 * N_PE * i) for i in range(2)]; o += 2 * 256 * N_PE
    sig = [sb("sig%d" % i, [128, 512], F32, o + 2048 * i) for i in range(2)]; o += 4096
    wsl3 = [sb("wsl3_%d" % i, [128, KC, 256], BF16, o + 4096 * i) for i in range(2)]; o += 8192
    ydve = [sb("ydve%d" % i, [128, 512], F32, o + 2048 * i) for i in range(2)]; o += 4096
    if N_POOL:
        ypool = [sb("ypool%d" % i, [128, 512], F32, o + 2048 * i) for i in range(2)]; o += 4096
        ptmp = sb("ptmp", [128, 512], F32, o); o += 2048
    B1_END = o
    assert o <= 212800, o

    for b in range(2):
        MEMSET("gpsimd", ubf[b][:, 0:14], 0.0, [], [("ubf", b)])
    rr = [0]

    def b1_load(c):
        b = c % 2
        DMA("gpsimd", wsl3[b][:, :, 0:128], wslab(w_in[:, c * 128:(c + 1) * 128]), [], [("wsl", b)])
        DMA("gpsimd", wsl3[b][:, :, 128:256], wslab(w_in[:, D + c * 128:D + (c + 1) * 128]), [], [("wsl", b)])
        for i_, j in enumerate(TAP_PE):
            ACT(Dg[b][:, i_, :], ident[:], AF.Copy, [], [("Dg", b)], scale=dwT[:, c * CW + j:c * CW + j + 1])

    def b1_glu(c, cg):
        b = c % 2
        c0, c1 = CG[cg]
        n = c1 - c0
        ba = nextbank()
        for kc in range(KC):
            MM(bank(ba)[:, :n], wsl3[b][:, kc, 0:128], uT[:, kc, c0:c1], kc == 0, kc == KC - 1, [("wsl", b)] + cgkeys(cg), [("ps", ba)])
        bg = nextbank()
        for kc in range(KC):
            MM(bank(bg)[:, :n], wsl3[b][:, kc, 128:256], uT[:, kc, c0:c1], kc == 0, kc == KC - 1, [("wsl", b)] + cgkeys(cg), [("ps", bg)])
        r = rr[0] % 2
        rr[0] += 1
        ACT(sig[r][:, :n], bank(bg)[:, :n], AF.Sigmoid, [("ps", bg)], [("sig", r)])
        TT("vector", ubf[b][:, 14 + c0:14 + c1], bank(ba)[:, :n], sig[r][:, :n], ALU.mult, [("ps", ba), ("sig", r)], [("ubf", b)])

    def b1_conv(c, g):
        b = c % 2
        r = (4 * c + g) % 2
        gs = slice(512 * g, 512 * g + 512)
        win = lambda j: ubf[b][:, 512 * g + j:512 * g + j + 512]
        tapw = lambda j: dwT[:, c * CW + j:c * CW + j + 1]
        by = nextbank()
        for i_, j in enumerate(TAP_PE):
            MM(bank(by)[:, :], Dg[b][:, i_, :], win(j), i_ == 0, i_ == N_PE - 1, [("Dg", b), ("ubf", b)], [("ps", by)])
        for i_, j in enumerate(TAP_POOL):
            if i_ == 0:
                TS("gpsimd", ypool[r][:], win(j), tapw(j), None, ALU.mult, None, [("ubf", b)], [("ypool", r)])
            else:
                TS("gpsimd", ptmp[:], win(j), tapw(j), None, ALU.mult, None, [("ubf", b)], ["ptmp"])
                TT("gpsimd", ypool[r][:], ypool[r][:], ptmp[:], ALU.add, [("ypool", r), "ptmp"], [("ypool", r)])
        for i_, j in enumerate(TAP_DVE):
            if i_ == 0:
                TS("vector", ydve[r][:], win(j), tapw(j), dw_b(c), ALU.mult, ALU.add, [("ubf", b)], [("ydve", r)])
            else:
                STT("vector", ydve[r][:], win(j), tapw(j), ydve[r][:], ALU.mult, ALU.add, [("ubf", b), ("ydve", r)], [("ydve", r)])
        if N_POOL:
            TT("vector", ydve[r][:], ydve[r][:], ypool[r][:], ALU.add, [("ydve", r), ("ypool", r)], [("ydve", r)])
        TT("vector", yc[:, c, gs], bank(by)[:, :], ydve[r][:], ALU.add, [("ps", by), ("ydve", r)], [("yc", c, g)])

    b1_load(0)
    for cg in range(5):
        b1_glu(0, cg)
    GLU_SLOT = {0: (0, 1), 1: (2,), 2: (3,), 3: (4,)}
    for c in range(KC):
        if c + 1 < KC:
            b1_load(c + 1)
        for g in range(4):
            if c + 1 < KC:
                for cg in GLU_SLOT[g]:
                    b1_glu(c + 1, cg)
            b1_conv(c, g)
    if debug:
        BARRIER()
        DMA("sync", dbg["yc"], yc[:], [], [])
    BARRIER()

    o = B1_END
    ycs = sb("ycs", [128, KC, SEQ], BF16, o); o += 32768
    YCS_END = o
    assert o <= 212800, o
    o = A_LOC
    tt_t = [sb("tt%d" % i, [128, 512], F32, o + 2048 * i) for i in range(2)]; o += 4096
    sqt = [sb("sqt%d" % i, [128, 512], BF16, o + 1024 * i) for i in range(2)]; o += 2048
    ycb = [sb("ycb%d" % i, [128, 512], BF16, o + 1024 * i) for i in range(2)]; o += 2048
    mean_t = [sb("mean_t%d" % i, [128, 512], F32, o + 2048 * i) for i in range(2)]; o += 4096
    msq_t = [sb("msq_t%d" % i, [128, 512], F32, o + 2048 * i) for i in range(2)]; o += 4096
    rstd_t = [sb("rstd_t%d" % i, [128, 512], F32, o + 2048 * i) for i in range(2)]; o += 4096
    assert o <= A_END

    o = A_END + 65536
    qp = [sb("qp%d" % i, [128, SEQ], BF16, o + 4096 * i) for i in range(2)]; o += 8192
    kp = [sb("kp%d" % i, [128, L], BF16, o + 4160 * i) for i in range(2)]; o += 8320
    vp = [sb("vp%d" % i, [128, 17, 128], BF16, o + 4352 * i) for i in range(2)]; o += 8704
    wq0 = [sb("wq0_%d" % k, [128, KC, 128], BF16, o + 2048 * k) for k in range(3)]; o += 6144
    assert o <= B1_END, o
    WQ1_OFF = A_LOC + 32768 + 12288 + 8192 + 8192 + 4096
    wq1 = [sb("wq1_%d" % k, [128, KC, 128], BF16, WQ1_OFF + 2048 * k) for k in range(3)]
    assert WQ1_OFF + 6144 <= A_END + 65536
    wq = [wq0, wq1]
    OB = 6
    PB = 7

    def proj_micro(fc):
        pb = fc % 2
        ops_ = []

        def W():
            for k in range(3):
                c_lo = (2 + k) * D + 128 * fc
                DMA("gpsimd", wq[pb][k][:], wslab(w_in[:, c_lo:c_lo + 128]), [], [("wq", pb, k)])
        ops_.append(W)

        def mm_chunk(outp, lhs_fn, rhs_fn, kcs, rkey):
            def f():
                for kc in kcs:
                    MM(outp, lhs_fn(kc), rhs_fn(kc), kc == 0, kc == KC - 1, [rkey], [("ps", PB)])
            return f

        for g in range(4):
            c0, c1 = CG[g + 1]
            for kcs in ((0, 1), (2, 3), (4, 5), (6, 7)):
                ops_.append(mm_chunk(bank(PB)[:, :], lambda kc: wq[pb][0][:, kc, :], lambda kc, c0=c0, c1=c1: uT[:, kc, c0:c1], kcs, ("wq", pb, 0)))
            ops_.append(lambda g=g: TS("vector", qp[pb][:, 512 * g:512 * g + 512], bank(PB)[:, :], 0.125, None, ALU.mult, None,
                                       [("ps", PB)], [("qp", pb)]))
        for cg in range(5):
            c0, c1 = CG[cg]
            n = c1 - c0
            for kcs in ((0, 1), (2, 3), (4, 5), (6, 7)):
                ops_.append(mm_chunk(bank(PB)[:, :n], lambda kc: wq[pb][1][:, kc, :], lambda kc, c0=c0, c1=c1: uT[:, kc, c0:c1], kcs, ("wq", pb, 1)))
            ops_.append(lambda c0=c0, c1=c1, n=n: CP("vector", kp[pb][:, c0:c1], bank(PB)[:, :n], [("ps", PB)], [("kp", pb)]))
        for vg in range(5):
            tiles = [16] if vg == 4 else list(range(4 * vg, 4 * vg + 4))
            for kb in tiles:
                rows = NM if kb == 16 else 128
                for kcs in ((0, 1, 2, 3), (4, 5, 6, 7)):
                    ops_.append(mm_chunk(bank(PB)[:rows, 128 * (kb % 4):128 * (kb % 4) + 128],
                                         lambda kc, kb=kb, rows=rows: uT[:, kc, 128 * kb:128 * kb + rows],
                                         lambda kc: wq[pb][2][:, kc, :], kcs, ("wq", pb, 2)))
            if vg == 4:
                ops_.append(lambda: CP("vector", vp[pb][:NM, 16, :], bank(PB)[:NM, 0:128], [("ps", PB)], [("vp", pb)]))
            else:
                ops_.append(lambda vg=vg: CP("vector", vp[pb][:, 4 * vg:4 * vg + 4, :], bank(PB)[:, :].rearrange("p (t n) -> p t n", t=4),
                                             [("ps", PB)], [("vp", pb)]))
        return ops_

    pend0 = proj_micro(0)

    def drip0(k):
        for _ in range(k):
            if pend0:
                pend0.pop(0)()

    def B2_stats(g):
        gs = slice(512 * g, 512 * g + 512)
        q_ = g % 2
        bs, bq = 2 * q_, 2 * q_ + 1
        for c in range(KC):
            r = c % 2
            ACT(sqt[r][:], yc[:, c, gs], AF.Square, [], [("sqt", r)])
            CP("vector", ycb[r][:], yc[:, c, gs], [], [("ycb", r)])
            MM(bank(bs)[:, :], negones[:], ycb[r][:], c == 0, c == KC - 1, [("ycb", r)], [("ps", bs)])
            MM(bank(bq)[:, :], negones[:], sqt[r][:], c == 0, c == KC - 1, [("sqt", r)], [("ps", bq)])
            drip0(1)

    def B2_stats_b(g):
        q_ = g % 2
        bs, bq = 2 * q_, 2 * q_ + 1
        TS("vector", mean_t[q_][:], bank(bs)[:, :], -1.0 / D, None, ALU.mult, None, [("ps", bs)], [("mean", q_)])
        TT("vector", msq_t[q_][:], mean_t[q_][:], mean_t[q_][:], ALU.mult, [("mean", q_)], [("msq", q_)])
        STT("vector", msq_t[q_][:], bank(bq)[:, :], -1.0 / D, msq_t[q_][:], ALU.mult, ALU.subtract, [("ps", bq), ("msq", q_)], [("msq", q_)])
        ACT(rstd_t[q_][:], msq_t[q_][:], AF.Ln, [("msq", q_)], [("rstdt", q_)], bias=epsl[:, :])
        ACT(rstd_t[q_][:], rstd_t[q_][:], AF.Exp, [("rstdt", q_)], [("rstdt", q_)], scale=-0.5)

    def B2_apply(g):
        gs = slice(512 * g, 512 * g + 512)
        q_ = g % 2
        for c in range(KC):
            r = c % 2
            TT("vector", tt_t[r][:], yc[:, c, gs], mean_t[q_][:], ALU.subtract, [("mean", q_)], [("tt", r)])
            TT("vector", tt_t[r][:], tt_t[r][:], rstd_t[q_][:], ALU.mult, [("tt", r), ("rstdt", q_)], [("tt", r)])
            ACT(ycs[:, c, gs], tt_t[r][:], AF.Silu, [("tt", r)], [("ycs", c, g)], scale=ln_g(c), bias=ln_b(c))
            drip0(1)

    B2_stats(0)
    B2_stats_b(0)
    for g in range(4):
        if g < 3:
            B2_stats(g + 1)
        B2_apply(g)
        if g < 3:
            B2_stats_b(g + 1)
    while pend0:
        pend0.pop(0)()
    if debug:
        BARRIER()
        DMA("sync", dbg["ycs"], ycs[:], [], [])
    BARRIER()

    o = A_LOC
    yaT = sb("yaT", [128, KC, SEQ], BF16, o); o += 32768
    YA_END = o
    et = [sb("et%d" % i, [128, 2, 512], F32, o + 4096 * i) for i in range(3)]; o += 12288
    spt = [sb("spt%d" % i, [128, 2, 512], BF16, o + 2048 * i) for i in range(4)]; o += 8192
    at = [sb("at%d" % i, [128, 2, 512], BF16, o + 2048 * i) for i in range(4)]; o += 8192
    St = [sb("St%d" % i, [128, 2, 512], BF16, o + 2048 * i) for i in range(2)]; o += 4096
    wq1_off = o; o += 6144
    assert o <= B1_END, o

    def ppv(i):
        return pp[i][:, :].rearrange("p (h n) -> p h n", h=2)

    blocks = []
    for pr_ in range(NH // 2):
        for g in range(4):
            lst = []
            for j in (4, 3, 2, 1, 0):
                lst.append(dict(kb=4 * g + j, np=(NM if j == 4 else 128), cs=(0 if j == 0 else 128 * j - 16), j=j))
            for kb in range(4 * g - 1, -1, -1):
                lst.append(dict(kb=kb, np=128, cs=0, j=None))
            for i, bl in enumerate(lst):
                bl.update(fc=pr_, g=g, first=(i == 0), last=(i == len(lst) - 1), hg=pr_ * 4 + g)
                blocks.append(bl)
    for n, bl in enumerate(blocks):
        bl["n"] = n

    def S1(bl):
        n, fc, g, kb, np_, cs = bl["n"], bl["fc"], bl["g"], bl["kb"], bl["np"], bl["cs"]
        pb = fc % 2
        z = n % 3
        part = bl["j"] is not None
        for hh in range(2):
            hp = hh * 64
            MM(ppv(z)[:np_, hh, cs:512], kp[pb][hp:hp + 64, 128 * kb:128 * kb + np_], qp[pb][hp:hp + 64, 512 * g + cs:512 * g + 512],
               True, not part, [("kp", pb), ("qp", pb)], [("pp", z)])
        if part:
            for hh in range(2):
                MM(ppv(z)[:np_, hh, cs:512], ident[:np_, :np_], MB[:np_, bl["j"], cs:512], False, True, [], [("pp", z)])

    def S2(bl):
        n, np_, cs = bl["n"], bl["np"], bl["cs"]
        z = n % 3
        ACT(et[n % 3][:np_, :, cs:512], ppv(z)[:np_, :, cs:512], AF.Exp, [("pp", z)], [("et", n % 3)])
        ACT(spt[n % 4][:np_, :, cs:512], et[n % 3][:np_, :, cs:512], AF.Ln, [("et", n % 3)], [("spt", n % 4)], bias=1.0)

    def S3(bl):
        n, np_, cs = bl["n"], bl["np"], bl["cs"]
        z = n % 3
        par = bl["hg"] % 2
        for hh in range(2):
            MM(ppv(z)[:np_, hh, cs:512], negtri[:np_, :np_], spt[n % 4][:np_, hh, cs:512], False, bl["first"],
               [("spt", n % 4)], [("pp", z)], skip=True)
        if not bl["first"]:
            for hh in range(2):
                MM(ppv(z)[:np_, hh, cs:512], negones[:, :np_], St[par][:, hh, cs:512], False, True, [("S", par)], [("pp", z)], skip=True)

    def S4(bl):
        n, np_, cs = bl["n"], bl["np"], bl["cs"]
        par = bl["hg"] % 2
        if bl["first"]:
            MEMSET("vector", St[par][:], 0.0, [], [("S", par)])
        if not bl["last"]:
            TT("vector", St[par][:np_, :, cs:512], St[par][:np_, :, cs:512], spt[n % 4][:np_, :, cs:512], ALU.add,
               [("S", par), ("spt", n % 4)], [("S", par)])

    def S5(bl):
        n, np_, cs = bl["n"], bl["np"], bl["cs"]
        z = n % 3
        ACT(at[n % 4][:np_, :, cs:512], ppv(z)[:np_, :, cs:512], AF.Exp, [("pp", z)], [("at", n % 4)])

    def S6(bl):
        n, fc, g, kb, np_, cs = bl["n"], bl["fc"], bl["g"], bl["kb"], bl["np"], bl["cs"]
        pb = fc % 2
        if bl["first"]:
            MM(bank(OB)[:, :], zeros[:, :], MB[:, 0, :], True, False, [], [("ps", OB)])
        for hh in range(2):
            hp = hh * 64
            MM(bank(OB)[hp:hp + 64, cs:512], vp[pb][:np_, kb, hp:hp + 64], at[n % 4][:np_, hh, cs:512], False, bl["last"],
               [("at", n % 4), ("vp", pb)], [("ps", OB)])
        if bl["last"]:
            CP("vector", yaT[:, fc, 512 * g:512 * g + 512], bank(OB)[:, :], [("ps", OB)], [("yaT", fc, g)])

    castjobs = []
    for s_ in range(2):
        castjobs.append((slabview(wo_bf, s_), wslab(w_o[:, 512 * s_:512 * s_ + 512]), ("wo_bf", s_)))
    for s_ in range(FC // 2):
        castjobs.append((slabview(wffin_bf, s_)[:, :, 0:256], wslab(w_ffn_in[:, 256 * s_:256 * s_ + 256]), ("wffin_bf", s_)))
        castjobs.append((slabview(wffin_bf, s_)[:, :, 256:512], wslab(w_ffn_in[:, DFF + 256 * s_:DFF + 256 * s_ + 256]), ("wffin_bf", s_)))
    NB = len(blocks)
    pend = []
    for step in range(NB + 2):
        if step < NB:
            bl = blocks[step]
            if bl["first"] and bl["g"] == 0:
                for _ in range(3):
                    if castjobs:
                        dst_, src_, key_ = castjobs.pop(0)
                        DMA("gpsimd", dst_, src_, [], [key_])
            if bl["first"] and bl["g"] == 0 and bl["fc"] + 1 < NH // 2:
                while pend:
                    pend.pop(0)()
                pend = proj_micro(bl["fc"] + 1)
            S1(bl)
            S2(bl)
        if 0 <= step - 1 < NB:
            S3(blocks[step - 1])
            S4(blocks[step - 1])
            S5(blocks[step - 1])
        if 0 <= step - 2 < NB:
            S6(blocks[step - 2])
        big = step < NB and (512 - blocks[step]["cs"]) >= 400
        for _ in range(3 if big else 0):
            if pend:
                pend.pop(0)()
    while pend:
        pend.pop(0)()
    assert not castjobs
    MIX_OFF = YA_END
    o = MIX_OFF + 32768
    wsb = [[sb("wsb%d_%d" % (i, k), [128, KC, 128], BF16, o + 2048 * (4 * i + k)) for k in range(4)] for i in range(2)]
    assert o + 2048 * 4 >= WQ1_OFF + 6144

    def b5_load(f, b):
        fs_ = slice(f * 128, (f + 1) * 128)
        DMA("gpsimd", wsb[b][0][:], wslab(w_in[:, 5 * D + f * 128:5 * D + (f + 1) * 128]), [], [("wsb", b, 0)])
        DMA("gpsimd", wsb[b][1][:], wslab(w_in[:, 6 * D + f * 128:6 * D + (f + 1) * 128]), [], [("wsb", b, 1)])
        DMA("gpsimd", wsb[b][2][:], wslab(w_conv_out[:, fs_]), [], [("wsb", b, 2)])
        DMA("gpsimd", wsb[b][3][:], wslab(w_attn_out[:, fs_]), [], [("wsb", b, 3)])

    b5_load(0, 1)
    if debug:
        BARRIER()
        DMA("sync", dbg["ya"], yaT[:], [], [])
    BARRIER()

    o = YA_END
    mixT = sb("mixT", [128, KC, SEQ], BF16, o); o += 32768
    o += 16384
    sg = [[sb("sg%d_%d" % (i, k), [128, 512], F32, o + 2048 * (2 * i + k)) for k in range(2)] for i in range(2)]; o += 8192
    t1 = [sb("t1_%d" % i, [128, 512], F32, o + 2048 * i) for i in range(2)]; o += 4096
    assert o <= B1_END
    rr = [0]
    for f in range(KC):
        b = (f + 1) % 2
        fs = slice(f * 128, (f + 1) * 128)
        if f > 0:
            b5_load(f, b)
        for g in range(4):
            gs = slice(512 * g, 512 * g + 512)
            us = slice(NM + 512 * g, NM + 512 * g + 512)
            r = rr[0] % 2
            rr[0] += 1
            bks = [4 * r + k for k in range(4)]
            srcs = [uT[:, :, us], uT[:, :, us], ycs[:, :, gs], yaT[:, :, gs]]
            for k in range(4):
                for kc in range(KC):
                    MM(bank(bks[k])[:, :], wsb[b][k][:, kc, :], srcs[k][:, kc, :], kc == 0, kc == KC - 1, [("wsb", b, k)], [("ps", bks[k])])
            ACT(sg[r][0][:], bank(bks[0])[:, :], AF.Sigmoid, [("ps", bks[0])], [("sg", r, 0)], bias=gate_b(f))
            ACT(sg[r][1][:], bank(bks[1])[:, :], AF.Sigmoid, [("ps", bks[1])], [("sg", r, 1)], bias=gate_b(8 + f))
            TT("vector", sg[r][0][:], bank(bks[2])[:, :], sg[r][0][:], ALU.mult, [("ps", bks[2]), ("sg", r, 0)], [("sg", r, 0)])
            TT("vector", sg[r][1][:], bank(bks[3])[:, :], sg[r][1][:], ALU.mult, [("ps", bks[3]), ("sg", r, 1)], [("sg", r, 1)])
            TT("vector", mixT[:, f, gs], sg[r][0][:], sg[r][1][:], ALU.add, [("sg", r, 0), ("sg", r, 1)], [("mixT", f, g)])
    B5_END = o
    o = B5_END
    wsl5 = [sb("wsl5_%d" % i, [128, KC, 512], BF16, o + 8192 * i) for i in range(2)]; o += 16384
    gbcA = sb("gbcA", [128, D], F32, o); o += 4096
    gbcB = sb("gbcB", [128, D], F32, o); o += 4096
    gbcC = sb("gbcC", [128, D], F32, o); o += 4096
    u2T0 = sb("u2T0", [128, KC, 512], BF16, o); o += 8192
    assert o <= B1_END, o
    DMA("sync", gbcA[:], g_postmix.partition_broadcast(128), [], ["gbcA"])
    DMA("sync", gbcB[:], g_preffn.partition_broadcast(128), [], ["gbcB"])
    DMA("sync", gbcC[:], g_postffn.partition_broadcast(128), [], ["gbcC"])
    pre_wo = {}
    for s_ in range(2):
        DMA("gpsimd", wsl5[s_][:], slabview(wo_bf, s_), [("wo_bf", s_)], [("wsl", s_)])
        pre_wo[(0, 0, s_)] = s_
    if debug:
        BARRIER()
        DMA("sync", dbg["mix"], mixT[:], [], [])
    BARRIER()

    o = MIX_OFF + 32768
    hT = sb("hT", [128, FC, 512], BF16, o); o += 22528
    sqj5 = sb("sqj5", [128, D], BF16, o); o += 2048
    sl = [sb("sl%d" % i, [128, 512], F32, o + 2048 * i) for i in range(2)]; o += 4096
    assert o <= B5_END, o
    o = B1_END
    h1 = [sb("h1_%d" % i, [128, 4, D], F32, o + 16384 * i) for i in range(2)]; o += 32768
    assert o <= 212800, o
    o = PERS
    wfo = sb("wfo", [128, FC, D], BF16, o); o += 45056
    tmpA = [sb("tmpA%d" % i, [128, D], F32, o + 4096 * i) for i in range(2)]; o += 8192
    u2b = [sb("u2b%d" % i, [128, D], BF16, o + 2048 * i) for i in range(2)]; o += 4096
    u2T1 = sb("u2T1", [128, KC, 512], BF16, o); o += 8192
    assert o <= MIX_OFF, o
    u2T = [u2T0, u2T1]

    def load_wfo(k0, k1):
        DMA("gpsimd", wfo[:, k0:k1, :], w_ffn_out[128 * k0:128 * k1, :].rearrange("(kc p) f -> p kc f", p=128), [], ["wfo"])

    cnt5 = dict(t=0, a=0, w=2)

    def rms_rstd(src_ap, c_, reads):
        ACT(sqj5[:], src_ap, AF.Square, reads, ["sqj5", ("st", c_)], accum_out=st[:, c_:c_ + 1])
        ACT(st[:, 8 + c_:9 + c_], st[:, c_:c_ + 1], AF.Ln, [("st", c_)], [("st", 8 + c_)], scale=1.0 / D, bias=epsr[:, :])
        ACT(st[:, 16 + c_:17 + c_], st[:, 8 + c_:9 + c_], AF.Exp, [("st", 8 + c_)], [("st", 16 + c_)], scale=-0.5)

    def C_pieces(g):
        pg = g % 2
        pieces = []
        for half in range(2):
            tiles = (2 * half, 2 * half + 1)

            def Sstep(s_, tiles=tiles, half=half):
                if (g, half, s_) in pre_wo:
                    b = pre_wo[(g, half, s_)]
                else:
                    b = cnt5["w"] % 2
                    cnt5["w"] += 1
                    DMA("gpsimd", wsl5[b][:], slabview(wo_bf, s_), [("wo_bf", s_)], [("wsl", b)])
                for tt in tiles:
                    pi = 2 + tt % 2
                    for kc in range(KC):
                        MM(pp[pi][:, 512 * s_:512 * s_ + 512], mixT[:, kc, 512 * g + 128 * tt:512 * g + 128 * tt + 128],
                           wsl5[b][:, kc, :], kc == 0, kc == KC - 1, [("wsl", b)], [("pp", pi)])

            def P1(tt):
                tk = 4 * g + tt
                i = tt % 2
                pi = 2 + i
                a = cnt5["a"] % 2
                cnt5["a"] += 1
                DMA("sync", h1[pg][:, tt, :], x[128 * tk:128 * tk + 128, :], [], [("h1", pg, tt)])
                c_ = 3 * i
                rms_rstd(pp[pi][:, :], c_, [("pp", pi)])
                STT("vector", tmpA[a][:], pp[pi][:, :], st[:, 16 + c_:17 + c_], gbcA[:], ALU.mult, ALU.mult,
                    [("pp", pi), ("st", 16 + c_), "gbcA"], [("tmpA", a)])
                TT("vector", h1[pg][:, tt, :], h1[pg][:, tt, :], tmpA[a][:], ALU.add, [("tmpA", a), ("h1", pg, tt)], [("h1", pg, tt)])

            def P2(tt):
                i = tt % 2
                c_ = 3 * i + 1
                rms_rstd(h1[pg][:, tt, :], c_, [("h1", pg, tt)])
                STT("vector", u2b[i][:], h1[pg][:, tt, :], st[:, 16 + c_:17 + c_], gbcB[:], ALU.mult, ALU.mult,
                    [("h1", pg, tt), ("st", 16 + c_), "gbcB"], [("u2b", i)])

            def P3(tt):
                i = tt % 2
                pi = 2 + i
                trv = pp[pi][:, 0:512].bitcast(BF16).rearrange("p (c n) -> p c n", c=8)
                for c in range(KC):
                    TR(trv[:, c, :], u2b[i][:, c * 128:(c + 1) * 128], ident[:], [("u2b", i)], [("pp", pi)])
                CP(evac_eng(), u2T[pg][:, :, 128 * tt:128 * tt + 128], trv[:, :, :], [("pp", pi)], [("u2T", pg, tt)])

            pieces.append(lambda Sstep=Sstep: Sstep(0))
            pieces.append(lambda Sstep=Sstep: Sstep(1))
            for fn in (P1, P2, P3):
                for tt in tiles:
                    pieces.append(lambda fn=fn, tt=tt: fn(tt))
        return pieces

    def D1_slab(g, s_):
        pg = g % 2
        allu2 = [("u2T", pg, t_) for t_ in range(4)]
        b = cnt5["w"] % 2
        cnt5["w"] += 1
        DMA("gpsimd", wsl5[b][:], slabview(wffin_bf, s_), [("wffin_bf", s_)], [("wsl", b)])
        for fi in range(2):
            hc = 2 * s_ + fi
            r = hc % 2
            ba, bb = 2 * r, 2 * r + 1
            for kc in range(KC):
                MM(bank(ba)[:, :], wsl5[b][:, kc, fi * 128:(fi + 1) * 128], u2T[pg][:, kc, :], kc == 0, kc == KC - 1,
                   [("wsl", b)] + allu2, [("ps", ba)])
            for kc in range(KC):
                MM(bank(bb)[:, :], wsl5[b][:, kc, 256 + fi * 128:256 + (fi + 1) * 128], u2T[pg][:, kc, :], kc == 0, kc == KC - 1,
                   [("wsl", b)] + allu2, [("ps", bb)])
            ACT(sl[r][:], bank(ba)[:, :], AF.Silu, [("ps", ba)], [("sl", r)])
            TT("vector", hT[:, hc, :], bank(bb)[:, :], sl[r][:], ALU.mult, [("ps", bb), ("sl", r)], [("hT", hc)])

    def D2E_tile(g, tt):
        pg = g % 2
        tk = 4 * g + tt
        allh = [("hT", hc) for hc in range(FC)]
        i = tt % 2
        pi = 2 + i
        a = cnt5["a"] % 2
        cnt5["a"] += 1
        for hc in range(FC):
            for s_ in range(2):
                MM(pp[pi][:, 512 * s_:512 * s_ + 512], hT[:, hc, 128 * tt:128 * tt + 128], wfo[:, hc, 512 * s_:512 * s_ + 512],
                   hc == 0, hc == FC - 1, ["wfo"] + allh, [("pp", pi)])
        c_ = 3 * i + 2
        rms_rstd(pp[pi][:, :], c_, [("pp", pi)])
        STT("vector", tmpA[a][:], pp[pi][:, :], st[:, 16 + c_:17 + c_], gbcC[:], ALU.mult, ALU.mult,
            [("pp", pi), ("st", 16 + c_), "gbcC"], [("tmpA", a)])
        TT("vector", tmpA[a][:], tmpA[a][:], h1[pg][:, tt, :], ALU.add, [("tmpA", a), ("h1", pg, tt)], [("tmpA", a)])
        DMA("sync", out[128 * tk:128 * tk + 128, :], tmpA[a][:], [("tmpA", a)], [])

    for pc in C_pieces(0):
        pc()
    for g in range(4):
        nxt = C_pieces(g + 1) if g < 3 else []
        for s_ in range(FC // 2):
            D1_slab(g, s_)
            if g == 0:
                load_wfo(2 * s_, 2 * s_ + 2)
            take = 2 if s_ < 6 else 1
            for _ in range(take):
                if nxt:
                    nxt.pop(0)()
        for tt in range(4):
            D2E_tile(g, tt)
            if nxt:
                nxt.pop(0)()
        while nxt:
            nxt.pop(0)()

    from contextlib import ExitStack
    with ExitStack() as es:
        csem = {e: es.enter_context(nc.semaphore("cs_" + e)) for e in ENGS}
        dsems = {"sync": [es.enter_context(nc.semaphore("ds%d" % i)) for i in range(8)],
                 "gpsimd": [es.enter_context(nc.semaphore("dg%d" % i)) for i in range(8)]}
        block = es.enter_context(nc.Block())
        run = P.emit(csem, dsems)

        @block.sync
        def _(e):
            run("sync", e)

        @block.scalar
        def _(e):
            run("scalar", e)

        @block.vector
        def _(e):
            run("vector", e)

        @block.gpsimd
        def _(e):
            run("gpsimd", e)

        @block.tensor
        def _(e):
            run("tensor", e)
    return nc, P


def make_in_maps(inputs):
    f = lambda a: np.ascontiguousarray(np.asarray(a, dtype=np.float32))
    x = f(inputs["x"])
    col = lambda v, n: f(v).reshape(n, 128).T
    pvec = np.ascontiguousarray(np.concatenate([
        col(inputs["gate_b"][0], 16), col(inputs["dw_b"][0], 8),
        col(inputs["conv_ln_g"][0], 8), col(inputs["conv_ln_b"][0], 8)], axis=1))
    dw = f(inputs["dw_w"][0])
    dwT = np.ascontiguousarray(dw.T.reshape(KC, 128, CW).transpose(1, 0, 2).reshape(128, KC * CW))
    shared = {
        "meta": f(inputs["meta_tokens"]),
        "w_in": f(inputs["w_in"][0]),
        "w_conv_out": f(inputs["w_conv_out"][0]),
        "w_attn_out": f(inputs["w_attn_out"][0]),
        "w_o": f(inputs["w_o"][0]),
        "w_ffn_in": f(inputs["w_ffn_in"][0]),
        "w_ffn_out": f(inputs["w_ffn_out"][0]),
        "g_pre": f(inputs["pre_mix_g"][0]),
        "g_postmix": f(inputs["post_mix_g"][0]),
        "g_preffn": f(inputs["pre_ffn_g"][0]),
        "g_postffn": f(inputs["post_ffn_g"][0]),
        "pvec": pvec,
        "dwT": dwT,
    }
    return [dict(shared, x=np.ascontiguousarray(x[b])) for b in range(x.shape[0])]


def kernel(**inputs):
    in_maps = make_in_maps(inputs)
    nc, _ = build_program(debug=False)
    res = run_bass_kernel_spmd(nc, in_maps, core_ids=list(range(NCORES)))
    return np.stack([np.asarray(r["out"], dtype=np.float32) for r in res.results], axis=0)
```

```python
import numpy as np
import concourse.bass as bass
import concourse.mybir as mybir
from concourse.bass_utils import run_bass_kernel_spmd

F32 = mybir.dt.float32
BF16 = mybir.dt.bfloat16
AF = mybir.ActivationFunctionType
ALU = mybir.AluOpType

D = 1024
SEQ = 2048
NM = 16
L = SEQ + NM
NH = 16
DH = 64
DFF = 2816
KC = D // 128
FC = DFF // 128
CW = 31
RMS_EPS = 1e-6
LN_EPS = 1e-5
NCORES = 8
ENGS = ("sync", "scalar", "vector", "gpsimd", "tensor")


class Prog:
    def __init__(self):
        self.ops = []

    def add(self, eng, fn, reads=(), writes=(), dma=False):
        self.ops.append((eng, fn, tuple(reads) + ("BAR",), tuple(writes), dma))

    def barrier(self, fn):
        self.ops.append(("gpsimd", fn, (), ("BAR",), False))

    def analyze(self):
        ops = self.ops
        last_w = {}
        readers = {}
        pos = []
        cnt = {e: 0 for e in ENGS}
        for (eng, _, _, _, _) in ops:
            pos.append(cnt[eng])
            cnt[eng] += 1
        final = []
        needed = set()
        for i, (eng, fn, reads, writes, dma) in enumerate(ops):
            deps = set()
            raw = set()
            for r in reads:
                if r in last_w:
                    deps.add(last_w[r])
                    raw.add(last_w[r])
            for w in writes:
                if w in last_w:
                    deps.add(last_w[w])
                deps.update(readers.get(w, ()))
            deps.discard(i)
            for r in reads:
                readers.setdefault(r, []).append(i)
            for w in writes:
                last_w[w] = i
                readers[w] = []
            best = {}
            keep = set()
            for d in deps:
                deng, _, _, _, ddma = ops[d]
                if ddma:
                    keep.add(d)
                    continue
                if deng == eng and not dma:
                    if d not in raw or eng == "tensor":
                        continue
                if deng not in best or pos[best[deng]] < pos[d]:
                    best[deng] = d
            keep.update(best.values())
            final.append(keep)
            needed |= keep
        self.final = final
        self.needed = needed

    def emit(self, csem, dsems):
        self.analyze()
        ops = self.ops
        sem_of = {}
        ccount = {e: 0 for e in ENGS}
        dcount = {e: [0] * len(dsems[e]) for e in dsems}
        dk = {e: 0 for e in dsems}
        dma_prev = {}
        for i, (eng, fn, reads, writes, dma) in enumerate(ops):
            if dma:
                pool = dsems[eng]
                s = dk[eng] % len(pool)
                dk[eng] += 1
                dma_prev[i] = (pool[s], dcount[eng][s])
                dcount[eng][s] += 16
                sem_of[i] = (pool[s], dcount[eng][s])
            elif i in self.needed:
                ccount[eng] += 1
                sem_of[i] = (csem[eng], ccount[eng])
        streams = {e: [] for e in ENGS}
        for i, op in enumerate(ops):
            streams[op[0]].append(i)
        self.stats = {e: len(streams[e]) for e in ENGS}

        def run(engname, eng):
            wd = {}
            for i in streams[engname]:
                _, fn, _, _, dma = ops[i]
                waits = {}
                for d in self.final[i]:
                    s, v = sem_of[d]
                    k = id(s)
                    if k not in waits or waits[k][1] < v:
                        waits[k] = (s, v)
                if dma:
                    s, v = dma_prev[i]
                    if v > 0:
                        k = id(s)
                        if k not in waits or waits[k][1] < v:
                            waits[k] = (s, v)
                for k, (s, v) in waits.items():
                    if wd.get(k, 0) >= v:
                        continue
                    wd[k] = v
                    eng.wait_ge(s, v)
                ins = fn(eng)
                if i in sem_of:
                    ins.then_inc(sem_of[i][0], 16 if dma else 1)
            if engname == "sync":
                for e in dsems:
                    for s, c in zip(dsems[e], dcount[e]):
                        if c > 0:
                            eng.wait_ge(s, c)
        return run


def build_program(debug=False):
    nc = bass.Bass("TRN2", target_bir_lowering=False)

    def din(name, shape, dt=F32):
        return nc.dram_tensor(name, list(shape), dt, kind="ExternalInput").ap()

    x = din("x", [SEQ, D])
    meta = din("meta", [NM, D])
    w_in = din("w_in", [D, 7 * D])
    w_conv_out = din("w_conv_out", [D, D])
    w_attn_out = din("w_attn_out", [D, D])
    w_o = din("w_o", [D, D])
    w_ffn_in = din("w_ffn_in", [D, 2 * DFF])
    w_ffn_out = din("w_ffn_out", [DFF, D])
    g_pre = din("g_pre", [D])
    g_postmix = din("g_postmix", [D])
    g_preffn = din("g_preffn", [D])
    g_postffn = din("g_postffn", [D])
    pvec_d = din("pvec", [128, 40])
    dwT_d = din("dwT", [128, KC * CW])
    out = nc.dram_tensor("out", [SEQ, D], F32, kind="ExternalOutput").ap()
    wffin_bf = nc.dram_tensor("wffin_bf", [FC // 2, 128, KC * 512], BF16, kind="Internal").ap()
    wo_bf = nc.dram_tensor("wo_bf", [2, 128, KC * 512], BF16, kind="Internal").ap()

    def slabview(t3, i):
        return t3[i].rearrange("p (kc j) -> p kc j", kc=KC)
    dbg = {}
    if debug:
        for nm in ("ya", "ycs", "mix"):
            dbg[nm] = nc.dram_tensor("dbg_" + nm, [128, KC, SEQ], BF16, kind="ExternalOutput").ap()
        dbg["uT"] = nc.dram_tensor("dbg_uT", [128, KC, L], BF16, kind="ExternalOutput").ap()
        dbg["q"] = nc.dram_tensor("dbg_q", [128, KC, SEQ], BF16, kind="ExternalOutput").ap()
        dbg["k"] = nc.dram_tensor("dbg_k", [128, KC, L], BF16, kind="ExternalOutput").ap()
        dbg["v"] = nc.dram_tensor("dbg_v", [128, 17, D], BF16, kind="ExternalOutput").ap()
        dbg["yc"] = nc.dram_tensor("dbg_yc", [128, KC, SEQ], F32, kind="ExternalOutput").ap()

    P = Prog()

    SB_BASE = 16512

    def sb(name, shape, dt, off):
        assert off + SB_BASE < 229376
        return nc.alloc_sbuf_tensor_at(name, list(shape), dt, offset=off + SB_BASE)

    o = 0
    ident = sb("ident", [128, 128], BF16, o); o += 256
    negtri = sb("negtri", [128, 128], BF16, o); o += 256
    negones = sb("negones", [128, 128], BF16, o); o += 256
    zeros = sb("zeros", [128, 128], BF16, o); o += 256
    onesf = sb("onesf", [128, 128], F32, o); o += 512
    MB = sb("MB", [128, 5, 512], BF16, o); o += 5120
    pvec = sb("pvec", [128, 40], F32, o); o += 160
    dwT = sb("dwT", [128, KC * CW], F32, o); o += 992
    ss = sb("ss", [128, 32], F32, o); o += 128
    lnt = sb("lnt", [128, 32], F32, o); o += 128
    rstd = sb("rstd", [128, 32], F32, o); o += 128
    st = sb("st", [128, 64], F32, o); o += 256
    epsr = sb("epsr", [128, 1], F32, o); o += 32
    epsl = sb("epsl", [128, 1], F32, o); o += 32
    barj = sb("barj", [128, 8], F32, o); o += 32
    assert o <= 9216
    PERS = 9216

    pp = [nc.alloc_psum_tensor("pp%d" % i, [128, 1024], F32) for i in range(4)]

    def bank(i):
        return pp[i // 2][:, (i % 2) * 512:(i % 2) * 512 + 512]

    def bank_bf(i):
        return pp[i // 2][:, (i % 2) * 512:(i % 2) * 512 + 512].bitcast(BF16).rearrange("p (c n) -> p c n", c=8)

    def MM(outp, lhsT, rhs, start, stop, reads, writes, skip=False):
        P.add("tensor", lambda e: e.matmul(outp, lhsT, rhs, start=start, stop=stop, skip_group_check=skip), reads, writes)

    def TR(outp, in_, idn, reads, writes):
        P.add("tensor", lambda e: e.transpose(outp, in_, idn), reads, writes)

    def ACT(outp, in_, func, reads, writes, **kw):
        P.add("scalar", lambda e: e.activation(out=outp, in_=in_, func=func, **kw), reads, writes)

    def TT(eng, outp, in0, in1, op, reads, writes):
        P.add(eng, lambda e: e.tensor_tensor(out=outp, in0=in0, in1=in1, op=op), reads, writes)

    def TS(eng, outp, in0, s1, s2, op0, op1, reads, writes):
        if s2 is None:
            P.add(eng, lambda e: e.tensor_scalar(out=outp, in0=in0, scalar1=s1, scalar2=None, op0=op0), reads, writes)
        else:
            P.add(eng, lambda e: e.tensor_scalar(out=outp, in0=in0, scalar1=s1, scalar2=s2, op0=op0, op1=op1), reads, writes)

    def STT(eng, outp, in0, scalar, in1, op0, op1, reads, writes):
        P.add(eng, lambda e: e.scalar_tensor_tensor(out=outp, in0=in0, scalar=scalar, in1=in1, op0=op0, op1=op1), reads, writes)

    def CP(eng, outp, in_, reads, writes):
        if eng == "scalar":
            ACT(outp, in_, AF.Copy, reads, writes)
        else:
            P.add(eng, lambda e: e.tensor_copy(out=outp, in_=in_), reads, writes)

    def DMA(eng, outp, in_, reads, writes):
        P.add(eng, lambda e: e.dma_start(out=outp, in_=in_), reads, writes, dma=True)

    def MEMSET(eng, ap, val, reads, writes):
        P.add(eng, lambda e: e.memset(ap, val), reads, writes)

    def ASEL(outp, in_, pattern, cmp, fill, base, cm, reads, writes):
        P.add("gpsimd", lambda e: e.affine_select(out=outp, in_=in_, pattern=pattern, compare_op=cmp, fill=fill,
                                                 base=base, channel_multiplier=cm), reads, writes)

    def BARRIER():
        P.barrier(lambda e: e.memset(barj[:], 0.0))

    def wslab(src2d):
        return src2d.rearrange("(kc p) f -> p kc f", p=128)

    MEMSET("gpsimd", ident[:], 1.0, [], ["c_ident"])
    ASEL(ident[:], ident[:], [[-1, 128]], ALU.is_equal, 0.0, 0, 1, ["c_ident"], ["c_ident"])
    MEMSET("gpsimd", epsr[:], RMS_EPS, [], ["c_eps"])
    MEMSET("gpsimd", epsl[:], LN_EPS, [], ["c_eps2"])
    MEMSET("gpsimd", ss[:], 0.0, [], ["c_ss"])
    MEMSET("gpsimd", st[:], 0.0, [], ["c_st"])
    DMA("sync", pvec[:], pvec_d, [], ["c_pvec"])
    DMA("sync", dwT[:], dwT_d, [], ["c_dwT"])
    BARRIER()

    gate_b = lambda f: pvec[:, f:f + 1]
    dw_b = lambda c: pvec[:, 16 + c:17 + c]
    ln_g = lambda c: pvec[:, 24 + c:25 + c]
    ln_b = lambda c: pvec[:, 32 + c:33 + c]

    def tile_rows(t):
        return NM if t == 0 else 128

    def tile_cols(t):
        return (0, NM) if t == 0 else (NM + 128 * (t - 1), NM + 128 * t)

    CG = [(0, NM)] + [(NM + 512 * i, NM + 512 * (i + 1)) for i in range(4)]

    evac_rr = [0]

    def evac_eng():
        evac_rr[0] += 1
        return "scalar" if evac_rr[0] % 2 else "vector"

    def phase_A(first, gbc, xt, xn, sqj, uT, trbanks):
        DMA("sync", gbc[:], g_pre.partition_broadcast(128), [], ["gbc"])

        def stage1(t):
            rows = tile_rows(t)
            b = t % len(xt)
            src = meta if t == 0 else x[128 * (t - 1):128 * t, :]
            DMA("sync", xt[b][:rows, :], src, [], [("xt", b)])
            if first:
                ACT(sqj[:rows, :], xt[b][:rows, :], AF.Square, [("xt", b)], ["sqj", ("ss", t)], accum_out=ss[:rows, t:t + 1])
                ACT(lnt[:rows, t:t + 1], ss[:rows, t:t + 1], AF.Ln, [("ss", t)], [("lnt", t)], scale=1.0 / D, bias=epsr[:rows, :])
                ACT(rstd[:rows, t:t + 1], lnt[:rows, t:t + 1], AF.Exp, [("lnt", t)], [("rstd", t)], scale=-0.5)
            STT("vector", xn[b][:rows, :], xt[b][:rows, :], rstd[:rows, t:t + 1], gbc[:rows, :], ALU.mult, ALU.mult,
                [("xt", b), ("rstd", t), "gbc"], [("xn", b)])

        def stage2(t):
            rows = tile_rows(t)
            c0, c1 = tile_cols(t)
            b = t % len(xt)
            tb = trbanks[t % len(trbanks)]
            for c in range(KC):
                TR(bank_bf(tb)[:, c, :rows], xn[b][:rows, c * 128:(c + 1) * 128], ident[:rows, :rows], [("xn", b)], [("ps", tb)])
            CP("vector", uT[:, :, c0:c1], bank_bf(tb)[:, :, :rows], [("ps", tb)], [("uT", t)])

        stage1(0)
        stage1(1)
        for t in range(17):
            if t + 2 < 17:
                stage1(t + 2)
            stage2(t)

    uT = sb("uT", [128, KC, L], BF16, PERS)
    o = PERS + 33024
    A_LOC = o
    gbc1 = sb("gbc1", [128, D], F32, o); o += 4096
    xt1 = [sb("xt1_%d" % i, [128, D], F32, o + 4096 * i) for i in range(4)]; o += 16384
    xn1 = [sb("xn1_%d" % i, [128, D], BF16, o + 2048 * i) for i in range(4)]; o += 8192
    sqj1 = sb("sqj1", [128, D], BF16, o); o += 2048
    A_END = o

    phase_A(True, gbc1, xt1, xn1, sqj1, uT, [6, 7])
    MEMSET("gpsimd", negtri[:], -1.0, [], ["c_negtri"])
    ASEL(negtri[:], negtri[:], [[-1, 128]], ALU.is_ge, 0.0, 0, 1, ["c_negtri"], ["c_negtri"])
    MEMSET("gpsimd", negones[:], -1.0, [], ["c_negones"])
    MEMSET("gpsimd", zeros[:], 0.0, [], ["c_zeros"])
    MEMSET("gpsimd", onesf[:], 1.0, [], ["c_onesf"])
    for j in range(5):
        MEMSET("gpsimd", MB[:, j, :], 0.0, [], [("c_MB", j)])
        ASEL(MB[:, j, :], MB[:, j, :], [[1, 512]], ALU.is_gt, -30000.0, 16 - 128 * j, -1, [("c_MB", j)], [("c_MB", j)])
    if debug:
        DMA("sync", dbg["uT"], uT[:], [("uT", t) for t in range(17)], [])
    allu = [("uT", t) for t in range(17)]

    def cgkeys(cg):
        return [("uT", 0)] if cg == 0 else [("uT", t) for t in range(4 * (cg - 1) + 1, 4 * cg + 1)]
    pr = [0]

    def nextbank(n=6):
        pr[0] = (pr[0] + 1) % n
        return pr[0]

    N_PE, N_DVE, N_POOL = 23, 8, 0
    assert N_PE + N_DVE + N_POOL == CW
    TAP_PE = list(range(0, N_PE))
    TAP_DVE = list(range(N_PE, N_PE + N_DVE))
    TAP_POOL = list(range(N_PE + N_DVE, CW))
    o = A_END
    yc = sb("yc", [128, KC, SEQ], F32, o); o += 65536
    UB = 14 + L
    ubf = [sb("ubf%d" % i, [128, UB], BF16, o + 4160 * i) for i in range(2)]; o += 8320
    Dg = [sb("Dg%d" % i, [128, N_PE, 128], BF16, o + 256 * N_PE * i) for i in range(2)]; o += 2 * 256 * N_PE
    sig = [sb("sig%d" % i, [128, 512], F32, o + 2048 * i) for i in range(2)]; o += 4096
    wsl3 = [sb("wsl3_%d" % i, [128, KC, 256], BF16, o + 4096 * i) for i in range(2)]; o += 8192
    ydve = [sb("ydve%d" % i, [128, 512], F32, o + 2048 * i) for i in range(2)]; o += 4096
    if N_POOL:
        ypool = [sb("ypool%d" % i, [128, 512], F32, o + 2048 * i) for i in range(2)]; o += 4096
        ptmp = sb("ptmp", [128, 512], F32, o); o += 2048
    B1_END = o
    assert o <= 212800, o

    for b in range(2):
        MEMSET("gpsimd", ubf[b][:, 0:14], 0.0, [], [("ubf", b)])
    rr = [0]

    def b1_load(c):
        b = c % 2
        DMA("gpsimd", wsl3[b][:, :, 0:128], wslab(w_in[:, c * 128:(c + 1) * 128]), [], [("wsl", b)])
        DMA("gpsimd", wsl3[b][:, :, 128:256], wslab(w_in[:, D + c * 128:D + (c + 1) * 128]), [], [("wsl", b)])
        for i_, j in enumerate(TAP_PE):
            ACT(Dg[b][:, i_, :], ident[:], AF.Copy, [], [("Dg", b)], scale=dwT[:, c * CW + j:c * CW + j + 1])

    def b1_glu(c, cg):
        b = c % 2
        c0, c1 = CG[cg]
        n = c1 - c0
        ba = nextbank()
        for kc in range(KC):
            MM(bank(ba)[:, :n], wsl3[b][:, kc, 0:128], uT[:, kc, c0:c1], kc == 0, kc == KC - 1, [("wsl", b)] + cgkeys(cg), [("ps", ba)])
        bg = nextbank()
        for kc in range(KC):
            MM(bank(bg)[:, :n], wsl3[b][:, kc, 128:256], uT[:, kc, c0:c1], kc == 0, kc == KC - 1, [("wsl", b)] + cgkeys(cg), [("ps", bg)])
        r = rr[0] % 2
        rr[0] += 1
        ACT(sig[r][:, :n], bank(bg)[:, :n], AF.Sigmoid, [("ps", bg)], [("sig", r)])
        TT("vector", ubf[b][:, 14 + c0:14 + c1], bank(ba)[:, :n], sig[r][:, :n], ALU.mult, [("ps", ba), ("sig", r)], [("ubf", b)])

    def b1_conv(c, g):
        b = c % 2
        r = (4 * c + g) % 2
        gs = slice(512 * g, 512 * g + 512)
        win = lambda j: ubf[b][:, 512 * g + j:512 * g + j + 512]
        tapw = lambda j: dwT[:, c * CW + j:c * CW + j + 1]
        by = nextbank()
        for i_, j in enumerate(TAP_PE):
            MM(bank(by)[:, :], Dg[b][:, i_, :], win(j), i_ == 0, i_ == N_PE - 1, [("Dg", b), ("ubf", b)], [("ps", by)])
        for i_, j in enumerate(TAP_POOL):
            if i_ == 0:
                TS("gpsimd", ypool[r][:], win(j), tapw(j), None, ALU.mult, None, [("ubf", b)], [("ypool", r)])
            else:
                TS("gpsimd", ptmp[:], win(j), tapw(j), None, ALU.mult, None, [("ubf", b)], ["ptmp"])
                TT("gpsimd", ypool[r][:], ypool[r][:], ptmp[:], ALU.add, [("ypool", r), "ptmp"], [("ypool", r)])
        for i_, j in enumerate(TAP_DVE):
            if i_ == 0:
                TS("vector", ydve[r][:], win(j), tapw(j), dw_b(c), ALU.mult, ALU.add, [("ubf", b)], [("ydve", r)])
            else:
                STT("vector", ydve[r][:], win(j), tapw(j), ydve[r][:], ALU.mult, ALU.add, [("ubf", b), ("ydve", r)], [("ydve", r)])
        if N_POOL:
            TT("vector", ydve[r][:], ydve[r][:], ypool[r][:], ALU.add, [("ydve", r), ("ypool", r)], [("ydve", r)])
        TT("vector", yc[:, c, gs], bank(by)[:, :], ydve[r][:], ALU.add, [("ps", by), ("ydve", r)], [("yc", c, g)])

    castjobs = []
    for s_ in range(2):
        castjobs.append((slabview(wo_bf, s_), wslab(w_o[:, 512 * s_:512 * s_ + 512]), ("wo_bf", s_)))
    for s_ in range(FC // 2):
        castjobs.append((slabview(wffin_bf, s_)[:, :, 0:256], wslab(w_ffn_in[:, 256 * s_:256 * s_ + 256]), ("wffin_bf", s_)))
        castjobs.append((slabview(wffin_bf, s_)[:, :, 256:512], wslab(w_ffn_in[:, DFF + 256 * s_:DFF + 256 * s_ + 256]), ("wffin_bf", s_)))
    b1_load(0)
    for cg in range(5):
        b1_glu(0, cg)
    GLU_SLOT = {0: (0, 1), 1: (2,), 2: (3,), 3: (4,)}
    for c in range(KC):
        if c + 1 < KC:
            b1_load(c + 1)
        for _ in range(3):
            if castjobs:
                dst_, src_, key_ = castjobs.pop(0)
                DMA("gpsimd", dst_, src_, [], [key_])
        for g in range(4):
            if c + 1 < KC:
                for cg in GLU_SLOT[g]:
                    b1_glu(c + 1, cg)
            b1_conv(c, g)
    if debug:
        BARRIER()
        DMA("sync", dbg["yc"], yc[:], [], [])
    BARRIER()

    o = B1_END
    ycs = sb("ycs", [128, KC, SEQ], BF16, o); o += 32768
    YCS_END = o
    assert o <= 212800, o
    o = A_LOC
    tt_t = [sb("tt%d" % i, [128, 512], F32, o + 2048 * i) for i in range(2)]; o += 4096
    sqt = [sb("sqt%d" % i, [128, 512], BF16, o + 1024 * i) for i in range(2)]; o += 2048
    ycb = [sb("ycb%d" % i, [128, 512], BF16, o + 1024 * i) for i in range(2)]; o += 2048
    mean_t = [sb("mean_t%d" % i, [128, 512], F32, o + 2048 * i) for i in range(2)]; o += 4096
    msq_t = [sb("msq_t%d" % i, [128, 512], F32, o + 2048 * i) for i in range(2)]; o += 4096
    rstd_t = [sb("rstd_t%d" % i, [128, 512], F32, o + 2048 * i) for i in range(2)]; o += 4096
    assert o <= A_END

    o = A_END + 65536
    qp = [sb("qp%d" % i, [128, SEQ], BF16, o + 4096 * i) for i in range(2)]; o += 8192
    kp = [sb("kp%d" % i, [128, L], BF16, o + 4160 * i) for i in range(2)]; o += 8320
    vp = [sb("vp%d" % i, [128, 17, 128], BF16, o + 4352 * i) for i in range(2)]; o += 8704
    wq0 = [sb("wq0_%d" % k, [128, KC, 128], BF16, o + 2048 * k) for k in range(3)]; o += 6144
    assert o <= B1_END, o
    WQ1_OFF = A_LOC + 32768 + 12288 + 8192 + 8192 + 4096
    wq1 = [sb("wq1_%d" % k, [128, KC, 128], BF16, WQ1_OFF + 2048 * k) for k in range(3)]
    assert WQ1_OFF + 6144 <= A_END + 65536
    wq = [wq0, wq1]
    OB = 6
    PB = 7

    def proj_micro(fc):
        pb = fc % 2
        ops_ = []

        def W():
            for k in range(3):
                c_lo = (2 + k) * D + 128 * fc
                DMA("gpsimd", wq[pb][k][:], wslab(w_in[:, c_lo:c_lo + 128]), [], [("wq", pb, k)])
        ops_.append(W)

        def mm_chunk(outp, lhs_fn, rhs_fn, kcs, rkey):
            def f():
                for kc in kcs:
                    MM(outp, lhs_fn(kc), rhs_fn(kc), kc == 0, kc == KC - 1, [rkey], [("ps", PB)])
            return f

        for g in range(4):
            c0, c1 = CG[g + 1]
            for kcs in ((0, 1), (2, 3), (4, 5), (6, 7)):
                ops_.append(mm_chunk(bank(PB)[:, :], lambda kc: wq[pb][0][:, kc, :], lambda kc, c0=c0, c1=c1: uT[:, kc, c0:c1], kcs, ("wq", pb, 0)))
            ops_.append(lambda g=g: TS("vector", qp[pb][:, 512 * g:512 * g + 512], bank(PB)[:, :], 0.125, None, ALU.mult, None,
                                       [("ps", PB)], [("qp", pb)]))
        for cg in range(5):
            c0, c1 = CG[cg]
            n = c1 - c0
            for kcs in ((0, 1), (2, 3), (4, 5), (6, 7)):
                ops_.append(mm_chunk(bank(PB)[:, :n], lambda kc: wq[pb][1][:, kc, :], lambda kc, c0=c0, c1=c1: uT[:, kc, c0:c1], kcs, ("wq", pb, 1)))
            ops_.append(lambda c0=c0, c1=c1, n=n: CP("vector", kp[pb][:, c0:c1], bank(PB)[:, :n], [("ps", PB)], [("kp", pb)]))
        for vg in range(5):
            tiles = [16] if vg == 4 else list(range(4 * vg, 4 * vg + 4))
            for kb in tiles:
                rows = NM if kb == 16 else 128
                for kcs in ((0, 1, 2, 3), (4, 5, 6, 7)):
                    ops_.append(mm_chunk(bank(PB)[:rows, 128 * (kb % 4):128 * (kb % 4) + 128],
                                         lambda kc, kb=kb, rows=rows: uT[:, kc, 128 * kb:128 * kb + rows],
                                         lambda kc: wq[pb][2][:, kc, :], kcs, ("wq", pb, 2)))
            if vg == 4:
                ops_.append(lambda: CP("vector", vp[pb][:NM, 16, :], bank(PB)[:NM, 0:128], [("ps", PB)], [("vp", pb)]))
            else:
                ops_.append(lambda vg=vg: CP("vector", vp[pb][:, 4 * vg:4 * vg + 4, :], bank(PB)[:, :].rearrange("p (t n) -> p t n", t=4),
                                             [("ps", PB)], [("vp", pb)]))
        return ops_

    pend0 = proj_micro(0)

    def drip0(k):
        for _ in range(k):
            if pend0:
                pend0.pop(0)()

    def B2_stats(g):
        gs = slice(512 * g, 512 * g + 512)
        q_ = g % 2
        bs, bq = 2 * q_, 2 * q_ + 1
        for c in range(KC):
            r = c % 2
            ACT(sqt[r][:], yc[:, c, gs], AF.Square, [], [("sqt", r)])
            CP("vector", ycb[r][:], yc[:, c, gs], [], [("ycb", r)])
            MM(bank(bs)[:, :], negones[:], ycb[r][:], c == 0, c == KC - 1, [("ycb", r)], [("ps", bs)])
            MM(bank(bq)[:, :], negones[:], sqt[r][:], c == 0, c == KC - 1, [("sqt", r)], [("ps", bq)])
            drip0(1)

    def B2_stats_b(g):
        q_ = g % 2
        bs, bq = 2 * q_, 2 * q_ + 1
        TS("vector", mean_t[q_][:], bank(bs)[:, :], -1.0 / D, None, ALU.mult, None, [("ps", bs)], [("mean", q_)])
        TT("vector", msq_t[q_][:], mean_t[q_][:], mean_t[q_][:], ALU.mult, [("mean", q_)], [("msq", q_)])
        STT("vector", msq_t[q_][:], bank(bq)[:, :], -1.0 / D, msq_t[q_][:], ALU.mult, ALU.subtract, [("ps", bq), ("msq", q_)], [("msq", q_)])
        ACT(rstd_t[q_][:], msq_t[q_][:], AF.Ln, [("msq", q_)], [("rstdt", q_)], bias=epsl[:, :])
        ACT(rstd_t[q_][:], rstd_t[q_][:], AF.Exp, [("rstdt", q_)], [("rstdt", q_)], scale=-0.5)

    def B2_apply(g):
        gs = slice(512 * g, 512 * g + 512)
        q_ = g % 2
        for c in range(KC):
            r = c % 2
            TT("vector", tt_t[r][:], yc[:, c, gs], mean_t[q_][:], ALU.subtract, [("mean", q_)], [("tt", r)])
            TT("vector", tt_t[r][:], tt_t[r][:], rstd_t[q_][:], ALU.mult, [("tt", r), ("rstdt", q_)], [("tt", r)])
            ACT(ycs[:, c, gs], tt_t[r][:], AF.Silu, [("tt", r)], [("ycs", c, g)], scale=ln_g(c), bias=ln_b(c))
            drip0(1)

    B2_stats(0)
    B2_stats_b(0)
    for g in range(4):
        if g < 3:
            B2_stats(g + 1)
        B2_apply(g)
        if g < 3:
            B2_stats_b(g + 1)
    while pend0:
        pend0.pop(0)()
    if debug:
        BARRIER()
        DMA("sync", dbg["ycs"], ycs[:], [], [])
    BARRIER()

    o = A_LOC
    yaT = sb("yaT", [128, KC, SEQ], BF16, o); o += 32768
    YA_END = o
    et = [sb("et%d" % i, [128, 2, 512], F32, o + 4096 * i) for i in range(3)]; o += 12288
    spt = [sb("spt%d" % i, [128, 2, 512], BF16, o + 2048 * i) for i in range(4)]; o += 8192
    at = [sb("at%d" % i, [128, 2, 512], BF16, o + 2048 * i) for i in range(4)]; o += 8192
    St = [sb("St%d" % i, [128, 2, 512], BF16, o + 2048 * i) for i in range(2)]; o += 4096
    wq1_off = o; o += 6144
    assert o <= B1_END, o

    def ppv(i):
        return pp[i][:, :].rearrange("p (h n) -> p h n", h=2)

    blocks = []
    for pr_ in range(NH // 2):
        for g in range(4):
            lst = []
            for j in (4, 3, 2, 1, 0):
                lst.append(dict(kb=4 * g + j, np=(NM if j == 4 else 128), cs=(0 if j == 0 else 128 * j - 16), j=j))
            for kb in range(4 * g - 1, -1, -1):
                lst.append(dict(kb=kb, np=128, cs=0, j=None))
            for i, bl in enumerate(lst):
                bl.update(fc=pr_, g=g, first=(i == 0), last=(i == len(lst) - 1), hg=pr_ * 4 + g)
                blocks.append(bl)
    for n, bl in enumerate(blocks):
        bl["n"] = n

    def S1(bl):
        n, fc, g, kb, np_, cs = bl["n"], bl["fc"], bl["g"], bl["kb"], bl["np"], bl["cs"]
        pb = fc % 2
        z = n % 3
        part = bl["j"] is not None
        for hh in range(2):
            hp = hh * 64
            MM(ppv(z)[:np_, hh, cs:512], kp[pb][hp:hp + 64, 128 * kb:128 * kb + np_], qp[pb][hp:hp + 64, 512 * g + cs:512 * g + 512],
               True, not part, [("kp", pb), ("qp", pb)], [("pp", z)])
        if part:
            for hh in range(2):
                MM(ppv(z)[:np_, hh, cs:512], ident[:np_, :np_], MB[:np_, bl["j"], cs:512], False, True, [], [("pp", z)])

    def S2(bl):
        n, np_, cs = bl["n"], bl["np"], bl["cs"]
        z = n % 3
        ACT(et[n % 3][:np_, :, cs:512], ppv(z)[:np_, :, cs:512], AF.Exp, [("pp", z)], [("et", n % 3)])
        ACT(spt[n % 4][:np_, :, cs:512], et[n % 3][:np_, :, cs:512], AF.Ln, [("et", n % 3)], [("spt", n % 4)], bias=1.0)

    def S3(bl):
        n, np_, cs = bl["n"], bl["np"], bl["cs"]
        z = n % 3
        par = bl["hg"] % 2
        for hh in range(2):
            MM(ppv(z)[:np_, hh, cs:512], negtri[:np_, :np_], spt[n % 4][:np_, hh, cs:512], False, bl["first"],
               [("spt", n % 4)], [("pp", z)], skip=True)
        if not bl["first"]:
            for hh in range(2):
                MM(ppv(z)[:np_, hh, cs:512], negones[:, :np_], St[par][:, hh, cs:512], False, True, [("S", par)], [("pp", z)], skip=True)

    def S4(bl):
        n, np_, cs = bl["n"], bl["np"], bl["cs"]
        par = bl["hg"] % 2
        if bl["first"]:
            MEMSET("gpsimd", St[par][:], 0.0, [], [("S", par)])
        if not bl["last"]:
            TT("vector", St[par][:np_, :, cs:512], St[par][:np_, :, cs:512], spt[n % 4][:np_, :, cs:512], ALU.add,
               [("S", par), ("spt", n % 4)], [("S", par)])

    def S5(bl):
        n, np_, cs = bl["n"], bl["np"], bl["cs"]
        z = n % 3
        ACT(at[n % 4][:np_, :, cs:512], ppv(z)[:np_, :, cs:512], AF.Exp, [("pp", z)], [("at", n % 4)])

    def S6(bl):
        n, fc, g, kb, np_, cs = bl["n"], bl["fc"], bl["g"], bl["kb"], bl["np"], bl["cs"]
        pb = fc % 2
        if bl["first"]:
            MM(bank(OB)[:, :], zeros[:, :], MB[:, 0, :], True, False, [], [("ps", OB)])
        for hh in range(2):
            hp = hh * 64
            MM(bank(OB)[hp:hp + 64, cs:512], vp[pb][:np_, kb, hp:hp + 64], at[n % 4][:np_, hh, cs:512], False, bl["last"],
               [("at", n % 4), ("vp", pb)], [("ps", OB)])
        if bl["last"]:
            CP("vector", yaT[:, fc, 512 * g:512 * g + 512], bank(OB)[:, :], [("ps", OB)], [("yaT", fc, g)])

    NB = len(blocks)
    pend = []
    for step in range(NB + 2):
        if step < NB:
            bl = blocks[step]
            if bl["first"] and bl["g"] == 0:
                for _ in range(3):
                    if castjobs:
                        dst_, src_, key_ = castjobs.pop(0)
                        DMA("gpsimd", dst_, src_, [], [key_])
            if bl["first"] and bl["g"] == 0 and bl["fc"] + 1 < NH // 2:
                while pend:
                    pend.pop(0)()
                pend = proj_micro(bl["fc"] + 1)
            S1(bl)
            S2(bl)
        if 0 <= step - 1 < NB:
            S3(blocks[step - 1])
            S4(blocks[step - 1])
            S5(blocks[step - 1])
        if 0 <= step - 2 < NB:
            S6(blocks[step - 2])
        big = step < NB and (512 - blocks[step]["cs"]) >= 400
        for _ in range(3 if big else 0):
            if pend:
                pend.pop(0)()
    while pend:
        pend.pop(0)()
    assert not castjobs
    MIX_OFF = YA_END
    o = MIX_OFF + 32768
    wsb = [[sb("wsb%d_%d" % (i, k), [128, KC, 128], BF16, o + 2048 * (4 * i + k)) for k in range(4)] for i in range(2)]
    assert o + 2048 * 4 >= WQ1_OFF + 6144

    def b5_load(f, b):
        fs_ = slice(f * 128, (f + 1) * 128)
        DMA("gpsimd", wsb[b][0][:], wslab(w_in[:, 5 * D + f * 128:5 * D + (f + 1) * 128]), [], [("wsb", b, 0)])
        DMA("gpsimd", wsb[b][1][:], wslab(w_in[:, 6 * D + f * 128:6 * D + (f + 1) * 128]), [], [("wsb", b, 1)])
        DMA("gpsimd", wsb[b][2][:], wslab(w_conv_out[:, fs_]), [], [("wsb", b, 2)])
        DMA("gpsimd", wsb[b][3][:], wslab(w_attn_out[:, fs_]), [], [("wsb", b, 3)])

    b5_load(0, 1)
    if debug:
        BARRIER()
        DMA("sync", dbg["ya"], yaT[:], [], [])
    BARRIER()

    o = YA_END
    mixT = sb("mixT", [128, KC, SEQ], BF16, o); o += 32768
    o += 16384
    sg = [[sb("sg%d_%d" % (i, k), [128, 512], F32, o + 2048 * (2 * i + k)) for k in range(2)] for i in range(2)]; o += 8192
    t1 = [sb("t1_%d" % i, [128, 512], F32, o + 2048 * i) for i in range(2)]; o += 4096
    assert o <= B1_END
    rr = [0]
    for f in range(KC):
        b = (f + 1) % 2
        fs = slice(f * 128, (f + 1) * 128)
        if f > 0:
            b5_load(f, b)
        for g in range(4):
            gs = slice(512 * g, 512 * g + 512)
            us = slice(NM + 512 * g, NM + 512 * g + 512)
            r = rr[0] % 2
            rr[0] += 1
            bks = [4 * r + k for k in range(4)]
            srcs = [uT[:, :, us], uT[:, :, us], ycs[:, :, gs], yaT[:, :, gs]]
            for k in range(4):
                for kc in range(KC):
                    MM(bank(bks[k])[:, :], wsb[b][k][:, kc, :], srcs[k][:, kc, :], kc == 0, kc == KC - 1, [("wsb", b, k)], [("ps", bks[k])])
            ACT(sg[r][0][:], bank(bks[0])[:, :], AF.Sigmoid, [("ps", bks[0])], [("sg", r, 0)], bias=gate_b(f))
            ACT(sg[r][1][:], bank(bks[1])[:, :], AF.Sigmoid, [("ps", bks[1])], [("sg", r, 1)], bias=gate_b(8 + f))
            TT("vector", sg[r][0][:], bank(bks[2])[:, :], sg[r][0][:], ALU.mult, [("ps", bks[2]), ("sg", r, 0)], [("sg", r, 0)])
            TT("vector", sg[r][1][:], bank(bks[3])[:, :], sg[r][1][:], ALU.mult, [("ps", bks[3]), ("sg", r, 1)], [("sg", r, 1)])
            TT("vector", mixT[:, f, gs], sg[r][0][:], sg[r][1][:], ALU.add, [("sg", r, 0), ("sg", r, 1)], [("mixT", f, g)])
    B5_END = o
    o = B5_END
    wsl5 = [sb("wsl5_%d" % i, [128, KC, 512], BF16, o + 8192 * i) for i in range(2)]; o += 16384
    gbcA = sb("gbcA", [128, D], F32, o); o += 4096
    gbcB = sb("gbcB", [128, D], F32, o); o += 4096
    gbcC = sb("gbcC", [128, D], F32, o); o += 4096
    u2T0 = sb("u2T0", [128, KC, 512], BF16, o); o += 8192
    assert o <= B1_END, o
    DMA("sync", gbcA[:], g_postmix.partition_broadcast(128), [], ["gbcA"])
    DMA("sync", gbcB[:], g_preffn.partition_broadcast(128), [], ["gbcB"])
    DMA("sync", gbcC[:], g_postffn.partition_broadcast(128), [], ["gbcC"])
    pre_wo = {}
    for s_ in range(2):
        DMA("gpsimd", wsl5[s_][:], slabview(wo_bf, s_), [("wo_bf", s_)], [("wsl", s_)])
        pre_wo[(0, 0, s_)] = s_
    if debug:
        BARRIER()
        DMA("sync", dbg["mix"], mixT[:], [], [])
    BARRIER()

    o = MIX_OFF + 32768
    hT = sb("hT", [128, FC, 512], BF16, o); o += 22528
    sqj5 = sb("sqj5", [128, D], BF16, o); o += 2048
    sl = [sb("sl%d" % i, [128, 512], F32, o + 2048 * i) for i in range(2)]; o += 4096
    assert o <= B5_END, o
    o = B1_END
    h1 = [sb("h1_%d" % i, [128, 4, D], F32, o + 16384 * i) for i in range(2)]; o += 32768
    assert o <= 212800, o
    o = PERS
    wfo = sb("wfo", [128, FC, D], BF16, o); o += 45056
    tmpA = [sb("tmpA%d" % i, [128, D], F32, o + 4096 * i) for i in range(2)]; o += 8192
    u2b = [sb("u2b%d" % i, [128, D], BF16, o + 2048 * i) for i in range(2)]; o += 4096
    u2T1 = sb("u2T1", [128, KC, 512], BF16, o); o += 8192
    assert o <= MIX_OFF, o
    u2T = [u2T0, u2T1]

    def load_wfo(k0, k1):
        DMA("gpsimd", wfo[:, k0:k1, :], w_ffn_out[128 * k0:128 * k1, :].rearrange("(kc p) f -> p kc f", p=128), [], ["wfo"])

    cnt5 = dict(t=0, a=0, w=2)

    def rms_rstd(src_ap, c_, reads):
        ACT(sqj5[:], src_ap, AF.Square, reads, ["sqj5", ("st", c_)], accum_out=st[:, c_:c_ + 1])
        ACT(st[:, 8 + c_:9 + c_], st[:, c_:c_ + 1], AF.Ln, [("st", c_)], [("st", 8 + c_)], scale=1.0 / D, bias=epsr[:, :])
        ACT(st[:, 16 + c_:17 + c_], st[:, 8 + c_:9 + c_], AF.Exp, [("st", 8 + c_)], [("st", 16 + c_)], scale=-0.5)

    def C_pieces(g):
        pg = g % 2
        pieces = []
        for half in range(2):
            tiles = (2 * half, 2 * half + 1)

            def Sstep(s_, tiles=tiles, half=half):
                if (g, half, s_) in pre_wo:
                    b = pre_wo[(g, half, s_)]
                else:
                    b = cnt5["w"] % 2
                    cnt5["w"] += 1
                    DMA("gpsimd", wsl5[b][:], slabview(wo_bf, s_), [("wo_bf", s_)], [("wsl", b)])
                for tt in tiles:
                    pi = 2 + tt % 2
                    for kc in range(KC):
                        MM(pp[pi][:, 512 * s_:512 * s_ + 512], mixT[:, kc, 512 * g + 128 * tt:512 * g + 128 * tt + 128],
                           wsl5[b][:, kc, :], kc == 0, kc == KC - 1, [("wsl", b)], [("pp", pi)])

            def P1(tt):
                tk = 4 * g + tt
                i = tt % 2
                pi = 2 + i
                a = cnt5["a"] % 2
                cnt5["a"] += 1
                DMA("sync", h1[pg][:, tt, :], x[128 * tk:128 * tk + 128, :], [], [("h1", pg, tt)])
                c_ = 3 * i
                rms_rstd(pp[pi][:, :], c_, [("pp", pi)])
                STT("vector", tmpA[a][:], pp[pi][:, :], st[:, 16 + c_:17 + c_], gbcA[:], ALU.mult, ALU.mult,
                    [("pp", pi), ("st", 16 + c_), "gbcA"], [("tmpA", a)])
                TT("vector", h1[pg][:, tt, :], h1[pg][:, tt, :], tmpA[a][:], ALU.add, [("tmpA", a), ("h1", pg, tt)], [("h1", pg, tt)])

            def P2(tt):
                i = tt % 2
                c_ = 3 * i + 1
                rms_rstd(h1[pg][:, tt, :], c_, [("h1", pg, tt)])
                STT("vector", u2b[i][:], h1[pg][:, tt, :], st[:, 16 + c_:17 + c_], gbcB[:], ALU.mult, ALU.mult,
                    [("h1", pg, tt), ("st", 16 + c_), "gbcB"], [("u2b", i)])

            def P3(tt):
                i = tt % 2
                pi = 2 + i
                trv = pp[pi][:, 0:512].bitcast(BF16).rearrange("p (c n) -> p c n", c=8)
                for c in range(KC):
                    TR(trv[:, c, :], u2b[i][:, c * 128:(c + 1) * 128], ident[:], [("u2b", i)], [("pp", pi)])
                CP(evac_eng(), u2T[pg][:, :, 128 * tt:128 * tt + 128], trv[:, :, :], [("pp", pi)], [("u2T", pg, tt)])

            pieces.append(lambda Sstep=Sstep: Sstep(0))
            pieces.append(lambda Sstep=Sstep: Sstep(1))
            for fn in (P1, P2, P3):
                for tt in tiles:
                    pieces.append(lambda fn=fn, tt=tt: fn(tt))
        return pieces

    def D1_slab(g, s_):
        pg = g % 2
        allu2 = [("u2T", pg, t_) for t_ in range(4)]
        b = cnt5["w"] % 2
        cnt5["w"] += 1
        DMA("gpsimd", wsl5[b][:], slabview(wffin_bf, s_), [("wffin_bf", s_)], [("wsl", b)])
        for fi in range(2):
            hc = 2 * s_ + fi
            r = hc % 2
            ba, bb = 2 * r, 2 * r + 1
            for kc in range(KC):
                MM(bank(ba)[:, :], wsl5[b][:, kc, fi * 128:(fi + 1) * 128], u2T[pg][:, kc, :], kc == 0, kc == KC - 1,
                   [("wsl", b)] + allu2, [("ps", ba)])
            for kc in range(KC):
                MM(bank(bb)[:, :], wsl5[b][:, kc, 256 + fi * 128:256 + (fi + 1) * 128], u2T[pg][:, kc, :], kc == 0, kc == KC - 1,
                   [("wsl", b)] + allu2, [("ps", bb)])
            ACT(sl[r][:], bank(ba)[:, :], AF.Silu, [("ps", ba)], [("sl", r)])
            TT("vector", hT[:, hc, :], bank(bb)[:, :], sl[r][:], ALU.mult, [("ps", bb), ("sl", r)], [("hT", hc)])

    def D2E_tile(g, tt):
        pg = g % 2
        tk = 4 * g + tt
        allh = [("hT", hc) for hc in range(FC)]
        i = tt % 2
        pi = 2 + i
        a = cnt5["a"] % 2
        cnt5["a"] += 1
        for hc in range(FC):
            for s_ in range(2):
                MM(pp[pi][:, 512 * s_:512 * s_ + 512], hT[:, hc, 128 * tt:128 * tt + 128], wfo[:, hc, 512 * s_:512 * s_ + 512],
                   hc == 0, hc == FC - 1, ["wfo"] + allh, [("pp", pi)])
        c_ = 3 * i + 2
        rms_rstd(pp[pi][:, :], c_, [("pp", pi)])
        STT("vector", tmpA[a][:], pp[pi][:, :], st[:, 16 + c_:17 + c_], gbcC[:], ALU.mult, ALU.mult,
            [("pp", pi), ("st", 16 + c_), "gbcC"], [("tmpA", a)])
        TT("vector", tmpA[a][:], tmpA[a][:], h1[pg][:, tt, :], ALU.add, [("tmpA", a), ("h1", pg, tt)], [("tmpA", a)])
        DMA("sync", out[128 * tk:128 * tk + 128, :], tmpA[a][:], [("tmpA", a)], [])

    for pc in C_pieces(0):
        pc()
    for g in range(4):
        nxt = C_pieces(g + 1) if g < 3 else []
        for s_ in range(FC // 2):
            D1_slab(g, s_)
            if g == 0:
                load_wfo(2 * s_, 2 * s_ + 2)
            take = 2 if s_ < 6 else 1
            for _ in range(take):
                if nxt:
                    nxt.pop(0)()
        for tt in range(4):
            D2E_tile(g, tt)
            if nxt:
                nxt.pop(0)()
        while nxt:
            nxt.pop(0)()

    from contextlib import ExitStack
    with ExitStack() as es:
        csem = {e: es.enter_context(nc.semaphore("cs_" + e)) for e in ENGS}
        dsems = {"sync": [es.enter_context(nc.semaphore("ds%d" % i)) for i in range(8)],
                 "gpsimd": [es.enter_context(nc.semaphore("dg%d" % i)) for i in range(8)]}
        block = es.enter_context(nc.Block())
        run = P.emit(csem, dsems)

        @block.sync
        def _(e):
            run("sync", e)

        @block.scalar
        def _(e):
            run("scalar", e)

        @block.vector
        def _(e):
            run("vector", e)

        @block.gpsimd
        def _(e):
            run("gpsimd", e)

        @block.tensor
        def _(e):
            run("tensor", e)
    return nc, P


def make_in_maps(inputs):
    f = lambda a: np.ascontiguousarray(np.asarray(a, dtype=np.float32))
    x = f(inputs["x"])
    col = lambda v, n: f(v).reshape(n, 128).T
    pvec = np.ascontiguousarray(np.concatenate([
        col(inputs["gate_b"][0], 16), col(inputs["dw_b"][0], 8),
        col(inputs["conv_ln_g"][0], 8), col(inputs["conv_ln_b"][0], 8)], axis=1))
    dw = f(inputs["dw_w"][0])
    dwT = np.ascontiguousarray(dw.T.reshape(KC, 128, CW).transpose(1, 0, 2).reshape(128, KC * CW))
    shared = {
        "meta": f(inputs["meta_tokens"]),
        "w_in": f(inputs["w_in"][0]),
        "w_conv_out": f(inputs["w_conv_out"][0]),
        "w_attn_out": f(inputs["w_attn_out"][0]),
        "w_o": f(inputs["w_o"][0]),
        "w_ffn_in": f(inputs["w_ffn_in"][0]),
        "w_ffn_out": f(inputs["w_ffn_out"][0]),
        "g_pre": f(inputs["pre_mix_g"][0]),
        "g_postmix": f(inputs["post_mix_g"][0]),
        "g_preffn": f(inputs["pre_ffn_g"][0]),
        "g_postffn": f(inputs["post_ffn_g"][0]),
        "pvec": pvec,
        "dwT": dwT,
    }
    return [dict(shared, x=np.ascontiguousarray(x[b])) for b in range(x.shape[0])]


def kernel(**inputs):
    in_maps = make_in_maps(inputs)
    nc, _ = build_program(debug=False)
    res = run_bass_kernel_spmd(nc, in_maps, core_ids=list(range(NCORES)))
    return np.stack([np.asarray(r["out"], dtype=np.float32) for r in res.results], axis=0)
```
